# Optimizing a Trainium2 kernel written in Bass

```python
import math
import jax, jax.numpy as jnp
from jax import lax
import numpy as np

D_MODEL = 2048
BATCH = 1
SEQ = 8192
DEPTH = 4

CHUNK = 64
Q_BLOCK = 128
NORM_EPS = 1e-6
N_BRANCH = 3

RET_HEAD_DIM = 128
RET_WIDTH = D_MODEL // 2
RET_HEADS = RET_WIDTH // RET_HEAD_DIM
ROPE_BASE = 10000.0

SB_HEAD_DIM = 128
SB_WIDTH = D_MODEL // 2
SB_HEADS = SB_WIDTH // SB_HEAD_DIM

SSD_HEAD_DIM = 64
SSD_WIDTH = D_MODEL // 2
SSD_HEADS = SSD_WIDTH // SSD_HEAD_DIM
SSD_GROUPS = 4
SSD_HEADS_PER_GROUP = SSD_HEADS // SSD_GROUPS
SSD_STATE = 128
SSD_CONV = 4
SSD_CONV_DIM = SSD_WIDTH + 2 * SSD_GROUPS * SSD_STATE

FFN_HIDDEN = ((8 * D_MODEL // 3 + 255) // 256) * 256

IN_SIZES = (RET_WIDTH, RET_WIDTH, RET_WIDTH, RET_WIDTH,
            SB_WIDTH, SB_WIDTH, SB_WIDTH,
            SSD_WIDTH, SSD_CONV_DIM, SSD_HEADS,
            N_BRANCH * D_MODEL)
IN_COLS = sum(IN_SIZES)

kernel_name = 'hybrid_retention_stickbreaking_ssd_trunk'


def rms_norm(x, w):
    xf = x.astype(jnp.float32)
    y = xf * lax.rsqrt(jnp.mean(xf * xf, axis=-1, keepdims=True) + NORM_EPS)
    return (y * w.astype(jnp.float32)).astype(x.dtype)


def apply_rotary(t, positions):
    half = t.shape[-1] // 2
    inv_freq = ROPE_BASE ** (-2.0 * jnp.arange(half, dtype=jnp.float32) / t.shape[-1])
    ang = positions.astype(jnp.float32)[:, :, None] * inv_freq
    cos = jnp.cos(ang)[:, :, None, :]
    sin = jnp.sin(ang)[:, :, None, :]
    t1, t2 = t[..., :half], t[..., half:]
    return jnp.concatenate([t1 * cos - t2 * sin, t1 * sin + t2 * cos], axis=-1)


def retention_mixer(q, k, v, g, positions, gn_w):
    out_dtype = q.dtype
    b, s, _ = q.shape
    nc = s // CHUNK
    H, Dh = RET_HEADS, RET_HEAD_DIM
    q = apply_rotary(q.astype(jnp.float32).reshape(b, s, H, Dh), positions)
    k = apply_rotary(k.astype(jnp.float32).reshape(b, s, H, Dh), positions) * (Dh ** -0.5)
    v = v.astype(jnp.float32).reshape(b, s, H, Dh)
    log_gamma = jnp.log1p(-jnp.exp2(-5.0 - jnp.arange(H, dtype=jnp.float32)))
    idx = jnp.arange(CHUNK, dtype=jnp.float32)
    intra_decay = jnp.exp(log_gamma[:, None, None] * jnp.abs(idx[:, None] - idx[None, :]))
    q = q.reshape(b, nc, CHUNK, H, Dh)
    k = k.reshape(b, nc, CHUNK, H, Dh)
    v = v.reshape(b, nc, CHUNK, H, Dh)
    scores = jnp.einsum('bcihd,bcjhd->bchij', q, k) * intra_decay
    o_intra = jnp.einsum('bchij,bcjhd->bcihd', scores, v)
    q_decay = jnp.exp(log_gamma[None, :] * (idx[:, None] + 1.0))
    k_decay = jnp.exp(log_gamma[None, :] * (CHUNK - 1.0 - idx[:, None]))
    chunk_decay = jnp.exp(log_gamma * CHUNK)

    def step(state, inp):
        qc, kc, vc = inp
        cross = jnp.einsum('bihd,bhde->bihe', qc * q_decay[None, :, :, None], state)
        state = state * chunk_decay[None, :, None, None] + jnp.einsum(
            'bjhd,bjhe->bhde', kc * k_decay[None, :, :, None], vc)
        return state, cross

    state0 = jnp.zeros((b, H, Dh, Dh), jnp.float32)
    _, o_cross = lax.scan(step, state0, (jnp.moveaxis(q, 1, 0), jnp.moveaxis(k, 1, 0), jnp.moveaxis(v, 1, 0)))
    o = (o_intra + jnp.moveaxis(o_cross, 0, 1)).reshape(b, s, H, Dh)
    mu = jnp.mean(o, axis=-1, keepdims=True)
    var = jnp.mean(jnp.square(o - mu), axis=-1, keepdims=True)
    o = (o - mu) * lax.rsqrt(var + NORM_EPS) * gn_w.astype(jnp.float32).reshape(H, Dh)
    o = o.reshape(b, s, RET_WIDTH) * jax.nn.silu(g.astype(jnp.float32))
    return o.astype(out_dtype)


def stick_breaking_mixer(q, k, v):
    b, s, _ = q.shape
    H, Dh = SB_HEADS, SB_HEAD_DIM
    q = q.reshape(b, s, H, Dh).transpose(0, 2, 1, 3)
    k = k.reshape(b, s, H, Dh).transpose(0, 2, 1, 3)
    v = v.reshape(b, s, H, Dh).transpose(0, 2, 1, 3)
    key_pos = jnp.arange(s)
    scale = Dh ** -0.5

    def block(i):
        qb = lax.dynamic_slice_in_dim(q, i * Q_BLOCK, Q_BLOCK, axis=2)
        z = jnp.einsum('bhqd,bhkd->bhqk', qb, k).astype(jnp.float32) * scale
        t = i * Q_BLOCK + jnp.arange(Q_BLOCK)
        mask = key_pos[None, :] < t[:, None]
        log_keep = jnp.where(mask, jax.nn.log_sigmoid(-z), 0.0)
        later = lax.cumsum(log_keep, axis=3, reverse=True) - log_keep
        w = jnp.where(mask, jnp.exp(jax.nn.log_sigmoid(z) + later), 0.0)
        return jnp.einsum('bhqk,bhkd->bhqd', w.astype(v.dtype), v)

    out = lax.map(block, jnp.arange(s // Q_BLOCK))
    return out.transpose(1, 0, 3, 2, 4).reshape(b, s, SB_WIDTH)


def causal_depthwise_conv(u, w, bias):
    out = lax.conv_general_dilated(
        u, w[:, None, :], window_strides=(1,), padding=[(SSD_CONV - 1, 0)],
        dimension_numbers=('NWC', 'WIO', 'NWC'), feature_group_count=u.shape[-1])
    return out + bias


def ssd_mixer(z, xbc, dt_raw, conv_w, conv_b, dt_bias, a_log, d_skip, norm_w):
    out_dtype = z.dtype
    b, s, _ = z.shape
    nc = s // CHUNK
    G, E, P, N = SSD_GROUPS, SSD_HEADS_PER_GROUP, SSD_HEAD_DIM, SSD_STATE
    f32 = jnp.float32
    xbc = jax.nn.silu(causal_depthwise_conv(xbc.astype(f32), conv_w.astype(f32), conv_b.astype(f32)))
    x = xbc[..., :SSD_WIDTH].reshape(b, nc, CHUNK, G, E, P)
    Bm = xbc[..., SSD_WIDTH:SSD_WIDTH + G * N].reshape(b, nc, CHUNK, G, N)
    Cm = xbc[..., SSD_WIDTH + G * N:].reshape(b, nc, CHUNK, G, N)
    dt = jax.nn.softplus(dt_raw.astype(f32) + dt_bias.astype(f32)).reshape(b, nc, CHUNK, G, E)
    A = -jnp.exp(a_log.astype(f32)).reshape(G, E)
    a = dt * A
    acum = jnp.cumsum(a, axis=2)
    xdt = x * dt[..., None]
    acum_t = jnp.moveaxis(acum, 2, -1)
    seg = acum_t[..., :, None] - acum_t[..., None, :]
    causal = jnp.tril(jnp.ones((CHUNK, CHUNK), dtype=bool))
    decay = jnp.exp(jnp.where(causal, seg, -jnp.inf))
    cb = jnp.einsum('bclgn,bcsgn->bcgls', Cm, Bm)
    y_diag = jnp.einsum('bcgls,bcgels,bcsgep->bclgep', cb, decay, xdt)
    decay_states = jnp.exp(acum[:, :, -1:] - acum)
    states = jnp.einsum('bclgn,bclge,bclgep->bcgepn', Bm, decay_states, xdt)
    chunk_decay = jnp.exp(acum[:, :, -1])

    def step(state, inp):
        c_c, acum_c, st_c, dec_c = inp
        y_off = jnp.einsum('blgn,bgepn,blge->blgep', c_c, state, jnp.exp(acum_c))
        state = state * dec_c[..., None, None] + st_c
        return state, y_off

    state0 = jnp.zeros((b, G, E, P, N), f32)
    _, y_off = lax.scan(step, state0, (jnp.moveaxis(Cm, 1, 0), jnp.moveaxis(acum, 1, 0),
                                       jnp.moveaxis(states, 1, 0), jnp.moveaxis(chunk_decay, 1, 0)))
    y = y_diag + jnp.moveaxis(y_off, 0, 1) + x * d_skip.astype(f32).reshape(G, E)[..., None]
    y = y.reshape(b, s, SSD_WIDTH)
    y = rms_norm(y * jax.nn.silu(z.astype(f32)), norm_w)
    return y.astype(out_dtype)


def hybrid_layer(x, positions, n_mix_pre, n_mix_post, n_ffn_pre, n_ffn_post, w_in, b_gate, ret_gn_w,
                 conv_w, conv_b, dt_bias, a_log, d_skip, ssd_norm_w, w_br_ret, w_br_sb, w_br_ssd,
                 w_out, w_gate, w_up, w_down):
    b, s, d = x.shape
    h = rms_norm(x, n_mix_pre)
    proj = h @ w_in
    split_points = np.cumsum(IN_SIZES)[:-1].tolist()
    rq, rk, rv, rg, sq, sk, sv, sz, sxbc, sdt, gate_logits = jnp.split(proj, split_points, axis=-1)
    y_ret = retention_mixer(rq, rk, rv, rg, positions, ret_gn_w)
    y_sb = stick_breaking_mixer(sq, sk, sv)
    y_ssd = ssd_mixer(sz, sxbc, sdt, conv_w, conv_b, dt_bias, a_log, d_skip, ssd_norm_w)
    gates = jax.nn.sigmoid(gate_logits + b_gate).reshape(b, s, N_BRANCH, d)
    merged = (gates[:, :, 0] * (y_ret @ w_br_ret)
              + gates[:, :, 1] * (y_sb @ w_br_sb)
              + gates[:, :, 2] * (y_ssd @ w_br_ssd))
    x = x + rms_norm(merged @ w_out, n_mix_post)
    h = rms_norm(x, n_ffn_pre)
    f = (jax.nn.silu(h @ w_gate) * (h @ w_up)) @ w_down
    return x + rms_norm(f, n_ffn_post)


def setup_inputs(seed: int = 0) -> dict:
    key = jax.random.key(seed)
    ks = jax.random.split(key, 24)
    f32 = jnp.float32

    def normal(k, shape, scale):
        return jax.random.normal(k, shape, f32) * scale

    def gain(k, shape):
        return 1.0 + 0.02 * jax.random.normal(k, shape, f32)

    dt_init = jnp.exp(jax.random.uniform(ks[10], (DEPTH, SSD_HEADS), f32)
                      * (math.log(0.1) - math.log(0.001)) + math.log(0.001))
    return {
        'x': jax.random.normal(ks[0], (BATCH, SEQ, D_MODEL), f32),
        'positions': jnp.broadcast_to(jnp.arange(SEQ, dtype=jnp.int32), (BATCH, SEQ)),
        'norm_mix_pre': gain(ks[1], (DEPTH, D_MODEL)),
        'norm_mix_post': gain(ks[2], (DEPTH, D_MODEL)),
        'norm_ffn_pre': gain(ks[3], (DEPTH, D_MODEL)),
        'norm_ffn_post': gain(ks[4], (DEPTH, D_MODEL)),
        'w_in': normal(ks[5], (DEPTH, D_MODEL, IN_COLS), D_MODEL ** -0.5),
        'b_gate': normal(ks[6], (DEPTH, N_BRANCH * D_MODEL), 0.01),
        'ret_gn_w': gain(ks[7], (DEPTH, RET_WIDTH)),
        'ssd_conv_w': normal(ks[8], (DEPTH, SSD_CONV, SSD_CONV_DIM), SSD_CONV ** -0.5),
        'ssd_conv_b': normal(ks[9], (DEPTH, SSD_CONV_DIM), 0.01),
        'ssd_dt_bias': dt_init + jnp.log(-jnp.expm1(-dt_init)),
        'ssd_a_log': jnp.log(jax.random.uniform(ks[11], (DEPTH, SSD_HEADS), f32, 1.0, 16.0)),
        'ssd_d': gain(ks[12], (DEPTH, SSD_HEADS)),
        'ssd_norm_w': gain(ks[13], (DEPTH, SSD_WIDTH)),
        'w_branch_ret': normal(ks[14], (DEPTH, RET_WIDTH, D_MODEL), RET_WIDTH ** -0.5),
        'w_branch_sb': normal(ks[15], (DEPTH, SB_WIDTH, D_MODEL), SB_WIDTH ** -0.5),
        'w_branch_ssd': normal(ks[16], (DEPTH, SSD_WIDTH, D_MODEL), SSD_WIDTH ** -0.5),
        'w_out': normal(ks[17], (DEPTH, D_MODEL, D_MODEL), D_MODEL ** -0.5),
        'ffn_w_gate': normal(ks[18], (DEPTH, D_MODEL, FFN_HIDDEN), D_MODEL ** -0.5),
        'ffn_w_up': normal(ks[19], (DEPTH, D_MODEL, FFN_HIDDEN), D_MODEL ** -0.5),
        'ffn_w_down': normal(ks[20], (DEPTH, FFN_HIDDEN, D_MODEL), FFN_HIDDEN ** -0.5),
    }


def reference(x, positions, norm_mix_pre, norm_mix_post, norm_ffn_pre, norm_ffn_post, w_in, b_gate,
              ret_gn_w, ssd_conv_w, ssd_conv_b, ssd_dt_bias, ssd_a_log, ssd_d, ssd_norm_w,
              w_branch_ret, w_branch_sb, w_branch_ssd, w_out, ffn_w_gate, ffn_w_up, ffn_w_down):
    for l in range(DEPTH):
        x = hybrid_layer(x, positions, norm_mix_pre[l], norm_mix_post[l], norm_ffn_pre[l], norm_ffn_post[l],
                         w_in[l], b_gate[l], ret_gn_w[l], ssd_conv_w[l], ssd_conv_b[l], ssd_dt_bias[l],
                         ssd_a_log[l], ssd_d[l], ssd_norm_w[l], w_branch_ret[l], w_branch_sb[l],
                         w_branch_ssd[l], w_out[l], ffn_w_gate[l], ffn_w_up[l], ffn_w_down[l])
    return x
```

```python
import contextlib, math
import numpy as np
from concourse.bass_utils import run_bass_kernel_spmd
import numpy as np
import concourse.bass as bass
import concourse.mybir as mybir

F32 = mybir.dt.float32
BF16 = mybir.dt.bfloat16
I32 = mybir.dt.int32
AF = mybir.ActivationFunctionType
ALU = mybir.AluOpType
AX = mybir.AxisListType


class Serial:
    def __init__(self, nc):
        self.nc = nc
        self.ops = []

    def op(self, eng, fn, dma=False):
        self.ops.append((eng, fn, 16 if dma else 1))

    def dma(self, out, in_, eng="sync"):
        self.op(eng, lambda e: e.dma_start(out=out, in_=in_), dma=True)

    def dma_cast(self, out, in_):
        self.op("gpsimd", lambda e: e.dma_start(out=out, in_=in_), dma=True)

    def mm(self, out, lhsT, rhs, start=True, stop=True):
        self.op("tensor", lambda e: e.matmul(out, lhsT, rhs, start=start, stop=stop))

    def act(self, out, in_, func, bias=None, scale=None):
        kw = {}
        if bias is not None:
            kw["bias"] = bias
        if scale is not None:
            kw["scale"] = scale
        self.op("scalar", lambda e: e.activation(out=out, in_=in_, func=func, **kw))

    def vec(self, name, *a, eng="vector", **kw):
        self.op(eng, lambda e: getattr(e, name)(*a, **kw))

    def emit(self):
        nc = self.nc
        per_eng = {"sync": [], "scalar": [], "tensor": [], "vector": [], "gpsimd": []}
        total = 0
        for eng, fn, inc in self.ops:
            per_eng[eng].append((fn, total, inc))
            total += inc
        final = total
        with nc.semaphore("G") as G, nc.Block() as block:
            def mk(name):
                lst = per_eng[name]

                def body(e):
                    for fn, before, inc in lst:
                        if before > 0:
                            e.wait_ge(G, before)
                        fn(e).then_inc(G, inc)
                    e.wait_ge(G, final)
                return body
            block.sync(mk("sync"))
            block.scalar(mk("scalar"))
            block.tensor(mk("tensor"))
            block.vector(mk("vector"))
            block.gpsimd(mk("gpsimd"))
        return final


D = 2048
KC = D // 128
EPS = 1e-6


def build_A(T, NM, NG):
    nc = bass.Bass("TRN2", target_bir_lowering=False)
    xT = nc.dram_tensor("xT", [D, T], F32, kind="ExternalInput").ap()
    gain = nc.dram_tensor("gain", [128, KC], F32, kind="ExternalInput").ap()
    wm = nc.dram_tensor("wm", [D, NM], F32, kind="ExternalInput").ap()
    wg = nc.dram_tensor("wg", [D, NG], F32, kind="ExternalInput").ap()
    bg = nc.dram_tensor("bg", [NG, 1], F32, kind="ExternalInput").ap()
    pm = nc.dram_tensor("pm", [NM, T], F32, kind="ExternalOutput").ap()
    pg = nc.dram_tensor("pg", [NG, T], F32, kind="ExternalOutput").ap()
    NH = T // 512
    with contextlib.ExitStack() as es:
        sb = lambda n, s, d: es.enter_context(nc.sbuf_tensor(n, s, d))
        ps = lambda n, s, d: es.enter_context(nc.psum_tensor(n, s, d))
        x32 = sb("x32", [128, KC, T], F32)
        sq = sb("sq", [128, KC, T], BF16)
        hT = sb("hT", [128, KC, T], BF16)
        g_sb = sb("g_sb", [128, KC], F32)
        ones = sb("ones", [128, 128], BF16)
        rstd = sb("rstd", [128, T], F32)
        wt = sb("wt", [128, KC, 128], BF16)
        o32 = sb("o32", [128, 512], F32)
        bcol = sb("bcol", [128, 1], F32)
        acc = ps("acc", [128, 512], F32)
        S = Serial(nc)
        S.dma(x32[:], xT.rearrange("(c p) t -> p c t", p=128))
        S.dma(g_sb[:], gain)
        S.vec("memset", ones[:], 1.0)
        S.act(sq[:], x32[:], AF.Square)
        for h in range(NH):
            ts = slice(h * 512, (h + 1) * 512)
            for c in range(KC):
                S.mm(acc[:], ones[:], sq[:, c, ts], start=(c == 0), stop=(c == KC - 1))
            S.vec("tensor_scalar", rstd[:, ts], acc[:], 1.0 / D, EPS, ALU.mult, ALU.add)
        S.act(rstd[:], rstd[:], AF.Sqrt)
        S.vec("reciprocal", rstd[:], rstd[:])
        for c in range(KC):
            S.vec("scalar_tensor_tensor", hT[:, c, :], x32[:, c, :], g_sb[:, c:c + 1], rstd[:], ALU.mult, ALU.mult)
        for (w, out, N, gate) in ((wm, pm, NM, False), (wg, pg, NG, True)):
            wv = w.rearrange("(c p) n -> p c n", p=128)
            for j0 in range(0, N, 128):
                wj = min(128, N - j0)
                S.dma_cast(wt[:, :, 0:wj], wv[:, :, j0:j0 + wj])
                if gate:
                    S.dma(bcol[0:wj, :], bg[j0:j0 + wj, :])
                for h in range(NH):
                    ts = slice(h * 512, (h + 1) * 512)
                    for c in range(KC):
                        S.mm(acc[0:wj, :], wt[:, c, 0:wj], hT[:, c, ts], start=(c == 0), stop=(c == KC - 1))
                    if gate:
                        S.act(o32[0:wj, :], acc[0:wj, :], AF.Sigmoid, bias=bcol[0:wj, :])
                    else:
                        S.act(o32[0:wj, :], acc[0:wj, :], AF.Copy)
                    S.dma(out[j0:j0 + wj, ts], o32[0:wj, :])
        S.emit()
    return nc


def sb_consts():
    j = np.arange(128)
    L = (j[:, None] >= j[None, :]).astype(np.float32)
    tl = np.arange(512)
    M = np.stack([(128 * r + j[:, None] < tl[None, :]).astype(np.float32) for r in range(4)], 1)
    return L, M


def emit_sb(nc, S_, es, qT, kT, v, Lc, Mc, yT, SEQ):
    sb = lambda n, s, d: es.enter_context(nc.sbuf_tensor(n, s, d))
    ps = lambda n, s, d: es.enter_context(nc.psum_tensor(n, s, d))
    NB = SEQ // 128
    NG = SEQ // 512
    scale = 128 ** -0.5
    st32 = sb("sb_st32", [128, 2048], F32)
    qb = sb("sb_qb", [128, SEQ], BF16)
    qnb = sb("sb_qnb", [128, SEQ], BF16)
    kb = sb("sb_kb", [128, SEQ], BF16)
    vb = sb("sb_vb", [128, NB, 128], BF16)
    Lb = sb("sb_L", [128, 128], BF16)
    onesb = sb("sb_ones", [128, 128], BF16)
    Mb = sb("sb_M", [128, 4, 512], BF16)
    carry = sb("sb_carry", [128, 512], F32)
    e32 = sb("sb_e", [128, 512], F32)
    spb = sb("sb_sp", [128, 512], BF16)
    tmp = sb("sb_tmp", [128, 512], F32)
    wb = sb("sb_w", [128, 512], BF16)
    o32 = sb("sb_o", [128, 512], F32)
    Z = ps("sb_Z", [128, 512], F32)
    C = ps("sb_C", [128, 512], F32)
    Tt = ps("sb_T", [128, 512], F32)
    O = ps("sb_O", [128, 512], F32)
    for t0 in range(0, SEQ, 2048):
        n = min(2048, SEQ - t0)
        S_.dma(st32[:, 0:n], qT[:, t0:t0 + n])
        S_.act(qb[:, t0:t0 + n], st32[:, 0:n], AF.Copy, scale=scale)
        S_.act(qnb[:, t0:t0 + n], st32[:, 0:n], AF.Copy, scale=-scale)
    S_.dma_cast(kb[:], kT)
    S_.dma_cast(vb[:], v.rearrange("(b p) d -> p b d", p=128))
    S_.dma_cast(Lb[:], Lc)
    S_.dma_cast(Mb[:], Mc)
    S_.vec("memset", onesb[:], 1.0)
    for G in range(NG):
        qs = slice(G * 512, (G + 1) * 512)
        S_.vec("memset", carry[:], 0.0)
        last = 4 * G + 3
        for kbi in range(last, -1, -1):
            ks = slice(kbi * 128, (kbi + 1) * 128)
            diag = kbi >= 4 * G
            r = kbi - 4 * G
            S_.mm(Z[:], kb[:, ks], qb[:, qs])
            S_.act(e32[:], Z[:], AF.Exp)
            S_.act(spb[:], e32[:], AF.Ln, bias=1.0)
            if diag:
                S_.vec("tensor_tensor", spb[:], spb[:], Mb[:, r, :], ALU.mult)
            S_.mm(C[:], Lb[:], spb[:], start=True, stop=False)
            S_.mm(C[:], kb[:, ks], qnb[:, qs], start=False, stop=True)
            S_.mm(Tt[:], onesb[:], spb[:])
            S_.vec("tensor_tensor", tmp[:], C[:], carry[:], ALU.add)
            S_.vec("tensor_tensor", carry[:], Tt[:], carry[:], ALU.add)
            S_.act(wb[:], tmp[:], AF.Exp, scale=-1.0)
            if diag:
                S_.vec("tensor_tensor", wb[:], wb[:], Mb[:, r, :], ALU.mult)
            S_.mm(O[:], vb[:, kbi, :], wb[:], start=(kbi == last), stop=(kbi == 0))
        S_.act(o32[:], O[:], AF.Copy)
        S_.dma(yT[:, qs], o32[:])


def build_sb(SEQ):
    nc = bass.Bass("TRN2", target_bir_lowering=False)
    qT = nc.dram_tensor("qT", [128, SEQ], F32, kind="ExternalInput").ap()
    kT = nc.dram_tensor("kT", [128, SEQ], F32, kind="ExternalInput").ap()
    v = nc.dram_tensor("v", [SEQ, 128], F32, kind="ExternalInput").ap()
    Lc = nc.dram_tensor("Lc", [128, 128], F32, kind="ExternalInput").ap()
    Mc = nc.dram_tensor("Mc", [128, 4, 512], F32, kind="ExternalInput").ap()
    yT = nc.dram_tensor("yT", [128, SEQ], F32, kind="ExternalOutput").ap()
    with contextlib.ExitStack() as es:
        S_ = Serial(nc)
        emit_sb(nc, S_, es, qT, kT, v, Lc, Mc, yT, SEQ)
        S_.emit()
    return nc


def ref_sb(q, k, v):
    s = q.shape[0]
    z = (q.astype(np.float64) @ k.astype(np.float64).T) * 128 ** -0.5
    mask = np.arange(s)[None, :] < np.arange(s)[:, None]
    lk = np.where(mask, -np.logaddexp(0, z), 0.0)
    later = np.cumsum(lk[:, ::-1], 1)[:, ::-1] - lk
    w = np.where(mask, np.exp(-np.logaddexp(0, -z) + later), 0.0)
    return w @ v.astype(np.float64)


EPS = 1e-6


def ret_consts(head):
    lg = np.log1p(-np.exp2(-5.0 - head)).astype(np.float32).astype(np.float64)
    sl = np.arange(128)[:, None]; tl = np.arange(512)[None, :]
    E0 = np.exp(lg * (tl - sl))
    Ed = []
    for r in range(4):
        valid = (2 * r + sl // 64) <= (tl // 64)
        Ed.append(np.where(valid, np.exp(lg * np.abs(tl - 128 * r - sl)), 0.0))
    tab = np.stack([E0] + Ed, 1).astype(np.float32)
    half = 64
    invf = (10000.0 ** (-2.0 * np.arange(half, dtype=np.float32) / 128)).astype(np.float32)
    invf = np.broadcast_to(invf[None, :], (128, half)).copy()
    ident = np.eye(128, dtype=np.float32)
    sc = np.broadcast_to(np.exp(lg * 128.0 * np.arange(64))[None, :], (128, 64)).astype(np.float32).copy()
    return tab, invf, ident, sc


def emit_ret(nc, S_, es, q, k, v, g, pos, gnw, tab, invf, ident, sc, y, SEQ):
    sb = lambda n, s, d: es.enter_context(nc.sbuf_tensor(n, s, d))
    ps = lambda n, s, d: es.enter_context(nc.psum_tensor(n, s, d))
    NB = SEQ // 128; NG = SEQ // 512
    NH_ = 2 if NB >= 8 else 1
    HB = NB // NH_
    qa = sb("rt_qa", [128, HB, 128], F32)
    cos = sb("rt_cos", [128, NB, 64], F32)
    sin = sb("rt_sin", [128, NB, 64], F32)
    t1 = sb("rt_t1", [128, NB, 64], F32)
    t2 = sb("rt_t2", [128, NB, 64], F32)
    rot = sb("rt_rot", [128, HB, 128], F32)
    rotb = sb("rt_rotb", [128, HB, 128], BF16)
    qTb = sb("rt_qTb", [128, SEQ], BF16)
    kTb = sb("rt_kTb", [128, SEQ], BF16)
    vb = sb("rt_vb", [128, NB, 128], BF16)
    posi = sb("rt_posi", [128, NB], I32)
    posf = sb("rt_posf", [128, NB], F32)
    invf_sb = sb("rt_invf", [128, 64], F32)
    identb = sb("rt_ident", [128, 128], BF16)
    tab_sb = sb("rt_tab", [128, 5, 512], F32)
    gnw_sb = sb("rt_gnw", [128, 128], F32)
    sc_sb = sb("rt_sc", [128, 64], F32)
    wt = sb("rt_wt", [128, 512], BF16)
    g_sb = sb("rt_g", [128, 4, 128], F32)
    cen = sb("rt_cen", [128, 128], F32)
    sq = sb("rt_sq", [128, 128], F32)
    st = sb("rt_st", [128, 4], F32)
    yo = sb("rt_yo", [128, 4, 128], F32)
    TP = ps("rt_TP", [128, 512], F32)
    Sc = ps("rt_Sc", [128, 512], F32)
    Ob = [ps(f"rt_O{j}", [128, 512], F32) for j in range(4)]
    S_.dma(posi[:], pos); S_.dma(invf_sb[:], invf); S_.dma(tab_sb[:], tab); S_.dma(gnw_sb[:], gnw); S_.dma(sc_sb[:], sc)
    S_.dma_cast(identb[:], ident)
    S_.dma_cast(vb[:], v.rearrange("(b p) d -> p b d", p=128))
    S_.vec("tensor_copy", posf[:], posi[:])
    for b in range(NB):
        S_.vec("tensor_scalar", t1[:, b, :], invf_sb[:], posf[:, b:b + 1], None, ALU.mult)
    MAGIC = 12582912.0
    for (dst, shift) in ((sin, 0.0), (cos, 0.5 * math.pi)):
        if shift:
            S_.vec("tensor_scalar", t1[:], t1[:], shift, None, ALU.add)
        S_.vec("tensor_scalar", t2[:], t1[:], 1.0 / (2 * math.pi), MAGIC, ALU.mult, ALU.add)
        S_.vec("tensor_scalar", t2[:], t2[:], -MAGIC, None, ALU.add)
        S_.vec("scalar_tensor_tensor", t2[:], t2[:], -2 * math.pi, t1[:], ALU.mult, ALU.add)
        S_.vec("tensor_scalar", t2[:], t2[:], math.pi, -math.pi, ALU.min, ALU.max)
        S_.act(dst[:], t2[:], AF.Sin)
    for (src, dstT, scl) in ((q, qTb, 1.0), (k, kTb, 128 ** -0.5)):
        for hb_ in range(NH_):
            b0 = hb_ * HB
            bs = slice(b0, b0 + HB)
            S_.dma(qa[:], src[b0 * 128:(b0 + HB) * 128, :].rearrange("(b p) d -> p b d", p=128))
            a1 = qa[:, :, 0:64]; a2 = qa[:, :, 64:128]
            u1 = t1[:, 0:HB, :]; u2 = t2[:, 0:HB, :]
            S_.vec("tensor_tensor", u1, a1, cos[:, bs, :], ALU.mult)
            S_.vec("tensor_tensor", u2, a2, sin[:, bs, :], ALU.mult)
            S_.vec("tensor_tensor", rot[:, :, 0:64], u1, u2, ALU.subtract)
            S_.vec("tensor_tensor", u1, a1, sin[:, bs, :], ALU.mult)
            S_.vec("tensor_tensor", u2, a2, cos[:, bs, :], ALU.mult)
            S_.vec("tensor_tensor", rot[:, :, 64:128], u1, u2, ALU.add)
            S_.act(rotb[:], rot[:], AF.Copy, scale=scl)
            for G in range(HB // 4):
                Gg = b0 // 4 + G
                for j in range(4):
                    S_.mm(TP[:, j * 128:(j + 1) * 128], rotb[:, 4 * G + j, :], identb[:])
                S_.act(dstT[:, Gg * 512:(Gg + 1) * 512], TP[:], AF.Copy)
    for G in range(NG):
        qs = slice(G * 512, (G + 1) * 512)
        last = 4 * G + 3
        for kbi in range(0, last + 1):
            ks = slice(kbi * 128, (kbi + 1) * 128)
            S_.mm(Sc[:], kTb[:, ks], qTb[:, qs])
            if kbi >= 4 * G:
                S_.vec("tensor_tensor", wt[:], Sc[:], tab_sb[:, 1 + kbi - 4 * G, :], ALU.mult)
            else:
                dlt = 4 * G - kbi
                S_.vec("scalar_tensor_tensor", wt[:], Sc[:], sc_sb[:, dlt:dlt + 1], tab_sb[:, 0, :], ALU.mult, ALU.mult)
            for j in range(4):
                S_.mm(Ob[j][:, 0:128], wt[:, j * 128:(j + 1) * 128], vb[:, kbi, :], start=(kbi == 0), stop=(kbi == last))
        S_.dma(g_sb[:], g[G * 512:(G + 1) * 512, :].rearrange("(b p) d -> p b d", p=128))
        S_.act(g_sb[:], g_sb[:], AF.Silu)
        for j in range(4):
            S_.vec("reduce_sum", st[:, 0:1], Ob[j][:, 0:128], AX.X)
            S_.vec("tensor_scalar", st[:, 0:1], st[:, 0:1], 1.0 / 128, None, ALU.mult)
            S_.vec("tensor_scalar", cen[:], Ob[j][:, 0:128], st[:, 0:1], None, ALU.subtract)
            S_.vec("tensor_tensor", sq[:], cen[:], cen[:], ALU.mult)
            S_.vec("reduce_sum", st[:, 1:2], sq[:], AX.X)
            S_.vec("tensor_scalar", st[:, 1:2], st[:, 1:2], 1.0 / 128, EPS, ALU.mult, ALU.add)
            S_.act(st[:, 2:3], st[:, 1:2], AF.Sqrt)
            S_.vec("reciprocal", st[:, 3:4], st[:, 2:3])
            S_.vec("scalar_tensor_tensor", cen[:], cen[:], st[:, 3:4], gnw_sb[:], ALU.mult, ALU.mult)
            S_.vec("tensor_tensor", yo[:, j, :], cen[:], g_sb[:, j, :], ALU.mult)
        S_.dma(y[G * 512:(G + 1) * 512, :].rearrange("(b p) d -> p b d", p=128), yo[:])


def build_ret(SEQ):
    nc = bass.Bass("TRN2", target_bir_lowering=False)
    NB = SEQ // 128
    di = lambda n, s, d=F32: nc.dram_tensor(n, s, d, kind="ExternalInput").ap()
    q = di("q", [SEQ, 128]); k = di("k", [SEQ, 128]); v = di("v", [SEQ, 128]); g = di("g", [SEQ, 128])
    pos = di("pos", [128, NB], I32); gnw = di("gnw", [128, 128]); tab = di("tab", [128, 5, 512])
    invf = di("invf", [128, 64]); ident = di("ident", [128, 128]); sc = di("sc", [128, 64])
    y = nc.dram_tensor("y", [SEQ, 128], F32, kind="ExternalOutput").ap()
    with contextlib.ExitStack() as es:
        S_ = Serial(nc)
        emit_ret(nc, S_, es, q, k, v, g, pos, gnw, tab, invf, ident, sc, y, SEQ)
        S_.emit()
    return nc


def ssd_consts():
    j = np.arange(128)
    tri = (j[:, None] <= j[None, :]).astype(np.float32)
    tl = np.arange(512)
    Mle = np.stack([(128 * r + j[:, None] <= tl[None, :]).astype(np.float32) for r in range(4)], 1)
    return tri, Mle, np.eye(128, dtype=np.float32)


def emit_ssd(nc, S_, es, xT, BT, CT, zT, dtr, cwx, cbx, cwB, cbB, cwC, cbC, dtb, alog, dsk, tri, Mle, ident, yT, SEQ):
    sb = lambda n, s, d: es.enter_context(nc.sbuf_tensor(n, s, d))
    ps = lambda n, s, d: es.enter_context(nc.psum_tensor(n, s, d))
    NB = SEQ // 128; NG = SEQ // 512
    up = sb("sd_up", [128, 3 + SEQ], F32)
    acc = sb("sd_acc", [128, SEQ], F32)
    xc = [sb(f"sd_xc{e}", [64, SEQ], F32) for e in range(2)]
    Bc = sb("sd_Bc", [128, SEQ], BF16)
    Cc = sb("sd_Cc", [128, SEQ], BF16)
    cw = sb("sd_cw", [128, 4], F32); cb = sb("sd_cb", [128, 1], F32)
    dt = sb("sd_dt", [128, NB, 2], F32); ev = sb("sd_ev", [128, NB, 2], F32)
    a = sb("sd_a", [128, NB, 2], F32); acum = sb("sd_acum", [128, NB, 2], F32); off = sb("sd_off", [128, NB, 2], F32)
    dtb_sb = sb("sd_dtb", [128, 2], F32); negA = sb("sd_negA", [128, 2], F32)
    dsk_sb = [sb(f"sd_dsk{e}", [64, 1], F32) for e in range(2)]
    tri_sb = sb("sd_tri", [128, 128], F32); ones32 = sb("sd_ones", [128, 128], F32); id32 = sb("sd_id", [128, 128], F32)
    Mle_sb = sb("sd_Mle", [128, 4, 512], F32)
    xdt = sb("sd_xdt", [128, NB, 128], BF16)
    dg = sb("sd_dg", [128, 128], F32)
    seg = sb("sd_seg", [128, 512], F32); wt = sb("sd_wt", [128, 512], BF16)
    zt = sb("sd_z", [64, 512], F32); yo = sb("sd_yo", [64, 512], F32)
    P1 = ps("sd_P1", [128, 512], F32)
    P2 = ps("sd_P2", [128, 512], F32)
    AR = [ps(f"sd_AR{e}", [128, 512], F32) for e in range(2)]
    Y = [ps(f"sd_Y{e}", [64, 512], F32) for e in range(2)]
    for (dst, src) in ((tri_sb, tri), (id32, ident), (Mle_sb, Mle), (dtb_sb, dtb), (negA, alog)):
        S_.dma(dst[:], src)
    S_.vec("memset", ones32[:], 1.0)
    S_.vec("memset", up[:, 0:3], 0.0)

    def conv(src, w, b, np_, lo):
        S_.dma(up[0:np_, 3:3 + SEQ], src[lo:lo + np_, :])
        S_.dma(cw[0:np_, :], w[lo:lo + np_, :]); S_.dma(cb[0:np_, :], b[lo:lo + np_, :])
        S_.vec("tensor_scalar", acc[0:np_, :], up[0:np_, 0:SEQ], cw[0:np_, 0:1], None, ALU.mult)
        for k in range(1, 4):
            S_.vec("scalar_tensor_tensor", acc[0:np_, :], up[0:np_, k:k + SEQ], cw[0:np_, k:k + 1], acc[0:np_, :], ALU.mult, ALU.add)
    for e in range(2):
        conv(xT, cwx, cbx, 64, 64 * e)
        S_.act(xc[e][:], acc[0:64, :], AF.Silu, bias=cb[0:64, :])
        S_.dma(dsk_sb[e][:], dsk[64 * e:64 * e + 64, :])
    conv(BT, cwB, cbB, 128, 0); S_.act(Bc[:], acc[:], AF.Silu, bias=cb[:])
    conv(CT, cwC, cbC, 128, 0); S_.act(Cc[:], acc[:], AF.Silu, bias=cb[:])
    S_.dma(dt[:], dtr)
    for e in range(2):
        S_.vec("tensor_scalar", dt[:, :, e], dt[:, :, e], dtb_sb[:, e:e + 1], None, ALU.add)
    S_.act(ev[:], dt[:], AF.Exp)
    S_.act(dt[:], ev[:], AF.Ln, bias=1.0)
    S_.act(negA[:], negA[:], AF.Exp)
    S_.vec("tensor_scalar", negA[:], negA[:], -1.0, None, ALU.mult)
    for e in range(2):
        S_.vec("tensor_scalar", a[:, :, e], dt[:, :, e], negA[:, e:e + 1], None, ALU.mult)
    af = a[:].rearrange("p b e -> p (b e)")
    S_.mm(P1[:, 0:NB * 2], tri_sb[:], af)
    S_.mm(P2[:, 0:NB * 2], ones32[:], af)
    S_.vec("tensor_copy", ev[:].rearrange("p b e -> p (b e)"), P2[:, 0:NB * 2])
    S_.vec("memset", off[:, 0, :], 0.0)
    for b in range(1, NB):
        S_.vec("tensor_tensor", off[:, b, :], off[:, b - 1, :], ev[:, b - 1, :], ALU.add)
    S_.vec("tensor_tensor", acum[:].rearrange("p b e -> p (b e)"), P1[:, 0:NB * 2], off[:].rearrange("p b e -> p (b e)"), ALU.add)
    for b in range(NB):
        for e in range(2):
            S_.mm(P1[:, 64 * e:64 * e + 64], xc[e][:, b * 128:(b + 1) * 128], id32[0:64, 0:64])
        for e in range(2):
            S_.vec("tensor_scalar", xdt[:, b, 64 * e:64 * e + 64], P1[:, 64 * e:64 * e + 64], dt[:, b, e:e + 1], None, ALU.mult)
    for G in range(NG):
        qs = slice(G * 512, (G + 1) * 512)
        last = 4 * G + 3
        for e in range(2):
            for j in range(4):
                S_.vec("tensor_scalar", dg[:], id32[:], acum[:, 4 * G + j, e:e + 1], None, ALU.mult)
                S_.mm(AR[e][:, j * 128:(j + 1) * 128], ones32[:], dg[:])
        for kbi in range(0, last + 1):
            ks = slice(kbi * 128, (kbi + 1) * 128)
            S_.mm(P1[:], Bc[:, ks], Cc[:, qs])
            for e in range(2):
                S_.vec("tensor_scalar", seg[:], AR[e][:], acum[:, kbi, e:e + 1], 0.0, ALU.subtract, ALU.min)
                S_.act(seg[:], seg[:], AF.Exp)
                if kbi >= 4 * G:
                    S_.vec("tensor_tensor", seg[:], seg[:], Mle_sb[:, kbi - 4 * G, :], ALU.mult)
                S_.vec("tensor_tensor", wt[:], P1[:], seg[:], ALU.mult)
                S_.mm(Y[e][:], xdt[:, kbi, 64 * e:64 * e + 64], wt[:], start=(kbi == 0), stop=(kbi == last))
        for e in range(2):
            S_.dma(zt[:], zT[64 * e:64 * e + 64, qs])
            S_.act(zt[:], zt[:], AF.Silu)
            S_.vec("scalar_tensor_tensor", yo[:], xc[e][:, qs], dsk_sb[e][:, 0:1], Y[e][:], ALU.mult, ALU.add)
            S_.vec("tensor_tensor", yo[:], yo[:], zt[:], ALU.mult)
            S_.dma(yT[64 * e:64 * e + 64, qs], yo[:])


def build_ssd(SEQ):
    nc = bass.Bass("TRN2", target_bir_lowering=False)
    NB = SEQ // 128
    di = lambda n, s, d=F32: nc.dram_tensor(n, s, d, kind="ExternalInput").ap()
    args = [di("xT", [128, SEQ]), di("BT", [128, SEQ]), di("CT", [128, SEQ]), di("zT", [128, SEQ]), di("dtr", [128, NB, 2]),
            di("cwx", [128, 4]), di("cbx", [128, 1]), di("cwB", [128, 4]), di("cbB", [128, 1]), di("cwC", [128, 4]), di("cbC", [128, 1]),
            di("dtb", [128, 2]), di("alog", [128, 2]), di("dsk", [128, 1]), di("tri", [128, 128]), di("Mle", [128, 4, 512]), di("ident", [128, 128])]
    yT = nc.dram_tensor("yT", [128, SEQ], F32, kind="ExternalOutput").ap()
    with contextlib.ExitStack() as es:
        S_ = Serial(nc)
        emit_ssd(nc, S_, es, *args, yT, SEQ)
        S_.emit()
    return nc


def ssd_inputs(c, z, xbc, dtraw, conv_w, conv_b, dt_bias, a_log, d_skip, SEQ):
    g = c // 2
    ch = slice(128 * c, 128 * c + 128); Bs = slice(1024 + 128 * g, 1024 + 128 * g + 128); Cs = slice(1536 + 128 * g, 1536 + 128 * g + 128)
    T = lambda a: np.ascontiguousarray(a.T)
    rep = lambda v: np.broadcast_to(v[None, :], (128, v.shape[0])).copy()
    tri, Mle, ident = ssd_consts()
    return {"xT": T(xbc[:, ch]), "BT": T(xbc[:, Bs]), "CT": T(xbc[:, Cs]), "zT": T(z[:, ch]),
            "dtr": np.ascontiguousarray(dtraw[:, 2 * c:2 * c + 2].reshape(SEQ // 128, 128, 2).transpose(1, 0, 2)),
            "cwx": T(conv_w[:, ch]), "cbx": conv_b[ch].reshape(128, 1).copy(), "cwB": T(conv_w[:, Bs]), "cbB": conv_b[Bs].reshape(128, 1).copy(),
            "cwC": T(conv_w[:, Cs]), "cbC": conv_b[Cs].reshape(128, 1).copy(),
            "dtb": rep(dt_bias[2 * c:2 * c + 2]), "alog": rep(a_log[2 * c:2 * c + 2]),
            "dsk": np.repeat(d_skip[2 * c:2 * c + 2], 64).reshape(128, 1).copy(), "tri": tri, "Mle": Mle, "ident": ident}


def _rstd_from(S_, ones, acc, sqt, nch, dn, rstd):
    for c in range(nch):
        S_.mm(acc[:], ones[:], sqt[:, c, :], start=(c == 0), stop=(c == nch - 1))
    S_.vec("tensor_scalar", rstd[:], acc[:], 1.0 / dn, 1e-6, ALU.mult, ALU.add)
    S_.act(rstd[:], rstd[:], AF.Sqrt)
    S_.vec("reciprocal", rstd[:], rstd[:])


def build_C1(T):
    nc = bass.Bass("TRN2", target_bir_lowering=False)
    di = lambda n, s, d=F32: nc.dram_tensor(n, s, d, kind="ExternalInput").ap()
    xT = di("xT", [D, T]); pg = di("pg", [3 * D, T])
    ys = [di("yr", [1024, T]), di("ys", [1024, T]), di("yd", [1024, T])]
    wbr = [di("wbr0", [1024, D]), di("wbr1", [1024, D]), di("wbr2", [1024, D])]
    wo = di("wo", [D, D]); nw = di("nw", [128, 8]); nmp = di("nmp", [128, KC])
    xo = nc.dram_tensor("xo", [D, T], F32, kind="ExternalOutput").ap()
    with contextlib.ExitStack() as es:
        sb = lambda n, s, d: es.enter_context(nc.sbuf_tensor(n, s, d))
        ps = lambda n, s, d: es.enter_context(nc.psum_tensor(n, s, d))
        x32 = sb("c1_x", [128, KC, 512], F32)
        yb = [sb(f"c1_y{i}", [128, 8, 512], BF16) for i in range(3)]
        sq = sb("c1_sq", [128, KC, 512], BF16)
        mg = sb("c1_mg", [128, KC, 512], BF16)
        m32 = sb("c1_m32", [128, 512], F32); tmp = sb("c1_tmp", [128, 512], F32); gt = sb("c1_gt", [128, 512], F32)
        o32 = sb("c1_o", [128, KC, 512], F32)
        wt = sb("c1_wt", [128, KC, 128], BF16)
        ones = sb("c1_ones", [128, 128], BF16)
        rstd = sb("c1_rstd", [128, 512], F32)
        nw_sb = sb("c1_nw", [128, 8], F32); nmp_sb = sb("c1_nmp", [128, KC], F32)
        acc = ps("c1_acc", [128, 512], F32); acc2 = ps("c1_acc2", [128, 512], F32)
        S_ = Serial(nc)
        S_.vec("memset", ones[:], 1.0)
        S_.dma(nw_sb[:], nw); S_.dma(nmp_sb[:], nmp)
        for h in range(T // 512):
            ts = slice(h * 512, (h + 1) * 512)
            S_.dma(x32[:], xT[:, ts].rearrange("(c p) t -> p c t", p=128))
            for i in range(3):
                S_.dma_cast(yb[i][:], ys[i][:, ts].rearrange("(c p) t -> p c t", p=128))
            S_.act(sq[:, 0:8, :], yb[2][:], AF.Square)
            _rstd_from(S_, ones, acc2, sq, 8, 1024.0, rstd)
            for c in range(8):
                S_.vec("scalar_tensor_tensor", yb[2][:, c, :], yb[2][:, c, :], nw_sb[:, c:c + 1], rstd[:], ALU.mult, ALU.mult)
            for j in range(KC):
                for i in range(3):
                    S_.dma_cast(wt[:, 0:8, :], wbr[i][:, j * 128:(j + 1) * 128].rearrange("(c p) n -> p c n", p=128))
                    for c in range(8):
                        S_.mm(acc[:], wt[:, c, :], yb[i][:, c, :], start=(c == 0), stop=(c == 7))
                    S_.dma(gt[:], pg[i * D + j * 128:i * D + (j + 1) * 128, ts])
                    if i == 0:
                        S_.vec("tensor_tensor", m32[:], acc[:], gt[:], ALU.mult)
                    else:
                        S_.vec("tensor_tensor", tmp[:], acc[:], gt[:], ALU.mult)
                        S_.vec("tensor_tensor", m32[:], m32[:], tmp[:], ALU.add)
                S_.act(mg[:, j, :], m32[:], AF.Copy)
            for j in range(KC):
                S_.dma_cast(wt[:], wo[:, j * 128:(j + 1) * 128].rearrange("(c p) n -> p c n", p=128))
                for c in range(KC):
                    S_.mm(acc[:], wt[:, c, :], mg[:, c, :], start=(c == 0), stop=(c == KC - 1))
                S_.act(o32[:, j, :], acc[:], AF.Copy)
            S_.act(sq[:], o32[:], AF.Square)
            _rstd_from(S_, ones, acc2, sq, KC, float(D), rstd)
            for c in range(KC):
                S_.vec("scalar_tensor_tensor", o32[:, c, :], o32[:, c, :], nmp_sb[:, c:c + 1], rstd[:], ALU.mult, ALU.mult)
                S_.vec("tensor_tensor", x32[:, c, :], x32[:, c, :], o32[:, c, :], ALU.add)
            S_.dma(xo[:, ts].rearrange("(c p) t -> p c t", p=128), x32[:])
        S_.emit()
    return nc


FH = 5632
FC = FH // 128


def build_C2(T):
    nc = bass.Bass("TRN2", target_bir_lowering=False)
    di = lambda n, s, d=F32: nc.dram_tensor(n, s, d, kind="ExternalInput").ap()
    xT = di("xT", [D, T]); wg = di("wg", [D, FH]); wu = di("wu", [D, FH]); wd = di("wd", [FH, D])
    nfp = di("nfp", [128, KC]); nfo = di("nfo", [128, KC])
    xo = nc.dram_tensor("xo", [D, T], F32, kind="ExternalOutput").ap()
    with contextlib.ExitStack() as es:
        sb = lambda n, s, d: es.enter_context(nc.sbuf_tensor(n, s, d))
        ps = lambda n, s, d: es.enter_context(nc.psum_tensor(n, s, d))
        x32 = sb("c2_x", [128, KC, 512], F32)
        sq = sb("c2_sq", [128, KC, 512], BF16)
        hT = sb("c2_h", [128, KC, 512], BF16)
        aT = sb("c2_a", [128, FC, 512], BF16)
        sg = sb("c2_sg", [128, 512], F32)
        o32 = sb("c2_o", [128, KC, 512], F32)
        wt = sb("c2_wt", [128, FC, 128], BF16)
        ones = sb("c2_ones", [128, 128], BF16)
        rstd = sb("c2_rstd", [128, 512], F32)
        nfp_sb = sb("c2_nfp", [128, KC], F32); nfo_sb = sb("c2_nfo", [128, KC], F32)
        acc = ps("c2_acc", [128, 512], F32); acc2 = ps("c2_acc2", [128, 512], F32)
        S_ = Serial(nc)
        S_.vec("memset", ones[:], 1.0)
        S_.dma(nfp_sb[:], nfp); S_.dma(nfo_sb[:], nfo)
        for h in range(T // 512):
            ts = slice(h * 512, (h + 1) * 512)
            S_.dma(x32[:], xT[:, ts].rearrange("(c p) t -> p c t", p=128))
            S_.act(sq[:], x32[:], AF.Square)
            _rstd_from(S_, ones, acc2, sq, KC, float(D), rstd)
            for c in range(KC):
                S_.vec("scalar_tensor_tensor", hT[:, c, :], x32[:, c, :], nfp_sb[:, c:c + 1], rstd[:], ALU.mult, ALU.mult)
            for m in range(FC):
                S_.dma_cast(wt[:, 0:KC, :], wg[:, m * 128:(m + 1) * 128].rearrange("(c p) n -> p c n", p=128))
                for c in range(KC):
                    S_.mm(acc[:], wt[:, c, :], hT[:, c, :], start=(c == 0), stop=(c == KC - 1))
                S_.act(sg[:], acc[:], AF.Silu)
                S_.dma_cast(wt[:, 0:KC, :], wu[:, m * 128:(m + 1) * 128].rearrange("(c p) n -> p c n", p=128))
                for c in range(KC):
                    S_.mm(acc2[:], wt[:, c, :], hT[:, c, :], start=(c == 0), stop=(c == KC - 1))
                S_.vec("tensor_tensor", aT[:, m, :], acc2[:], sg[:], ALU.mult)
            for j in range(KC):
                S_.dma_cast(wt[:], wd[:, j * 128:(j + 1) * 128].rearrange("(c p) n -> p c n", p=128))
                for m in range(FC):
                    S_.mm(acc[:], wt[:, m, :], aT[:, m, :], start=(m == 0), stop=(m == FC - 1))
                S_.act(o32[:, j, :], acc[:], AF.Copy)
            S_.act(sq[:], o32[:], AF.Square)
            _rstd_from(S_, ones, acc2, sq, KC, float(D), rstd)
            for c in range(KC):
                S_.vec("scalar_tensor_tensor", o32[:, c, :], o32[:, c, :], nfo_sb[:, c:c + 1], rstd[:], ALU.mult, ALU.mult)
                S_.vec("tensor_tensor", x32[:, c, :], x32[:, c, :], o32[:, c, :], ALU.add)
            S_.dma(xo[:, ts].rearrange("(c p) t -> p c t", p=128), x32[:])
        S_.emit()
    return nc


SEQ = 8192
NCORE = 8
TPC = SEQ // NCORE
NMIX = 10256
_PROG = {}


def _prog(name, fn):
    if name not in _PROG:
        _PROG[name] = fn()
    return _PROG[name]


def _run(nc, maps):
    return run_bass_kernel_spmd(nc, maps, core_ids=list(range(NCORE))).results


def _pc(v, n):
    return np.ascontiguousarray(np.asarray(v, np.float32).reshape(n, 128).T)


def kernel(x, positions, norm_mix_pre, norm_mix_post, norm_ffn_pre, norm_ffn_post, w_in, b_gate,
           ret_gn_w, ssd_conv_w, ssd_conv_b, ssd_dt_bias, ssd_a_log, ssd_d, ssd_norm_w,
           w_branch_ret, w_branch_sb, w_branch_ssd, w_out, ffn_w_gate, ffn_w_up, ffn_w_down):
    A = lambda a: np.asarray(a)
    C = np.ascontiguousarray
    x = A(x).astype(np.float32, copy=False)
    depth = A(w_in).shape[0]
    xT = C(x[0].T)
    xs = [C(xT[:, c * TPC:(c + 1) * TPC]) for c in range(NCORE)]
    pos_l = C(A(positions)[0].astype(np.int32).reshape(SEQ // 128, 128).T)
    Lc, Mc = sb_consts()
    rc = [ret_consts(c) for c in range(NCORE)]
    pA = _prog("A", lambda: build_A(TPC, NMIX, 3 * D))
    pSB = _prog("SB", lambda: build_sb(SEQ))
    pRT = _prog("RT", lambda: build_ret(SEQ))
    pSD = _prog("SD", lambda: build_ssd(SEQ))
    pC1 = _prog("C1", lambda: build_C1(TPC))
    pC2 = _prog("C2", lambda: build_C2(TPC))
    for l in range(depth):
        wl = A(w_in[l])
        wm = C(wl[:, :NMIX]); wg = C(wl[:, NMIX:])
        gain = _pc(norm_mix_pre[l], KC); bg = C(A(b_gate[l]).reshape(3 * D, 1))
        rA = _run(pA, [{"xT": xs[c], "gain": gain, "wm": wm, "wg": wg, "bg": bg} for c in range(NCORE)])
        pgs = [rA[c]["pg"] for c in range(NCORE)]
        pT = np.concatenate([rA[c]["pm"] for c in range(NCORE)], axis=1)
        del rA, wm, wg
        hb = lambda base, c: pT[base + 128 * c: base + 128 * (c + 1)]
        rS = _run(pSB, [{"qT": C(hb(4096, c)), "kT": C(hb(5120, c)), "v": C(hb(6144, c).T), "Lc": Lc, "Mc": Mc} for c in range(NCORE)])
        ysT = np.concatenate([rS[c]["yT"] for c in range(NCORE)], axis=0)
        gn = A(ret_gn_w[l])
        rR = _run(pRT, [{"q": C(hb(0, c).T), "k": C(hb(1024, c).T), "v": C(hb(2048, c).T), "g": C(hb(3072, c).T), "pos": pos_l,
                         "gnw": np.broadcast_to(gn[128 * c:128 * (c + 1)][None, :], (128, 128)).copy(),
                         "tab": rc[c][0], "invf": rc[c][1], "ident": rc[c][2], "sc": rc[c][3]} for c in range(NCORE)])
        yrT = np.concatenate([rR[c]["y"].T for c in range(NCORE)], axis=0)
        z_tm = pT[7168:8192].T; xbc_tm = pT[8192:10240].T; dt_tm = pT[10240:10256].T
        rD = _run(pSD, [ssd_inputs(c, z_tm, xbc_tm, dt_tm, A(ssd_conv_w[l]), A(ssd_conv_b[l]), A(ssd_dt_bias[l]),
                                   A(ssd_a_log[l]), A(ssd_d[l]), SEQ) for c in range(NCORE)])
        ydT = np.concatenate([rD[c]["yT"] for c in range(NCORE)], axis=0)
        del pT, rS, rR, rD
        tk = lambda a, c: C(a[:, c * TPC:(c + 1) * TPC])
        r1 = _run(pC1, [{"xT": xs[c], "pg": pgs[c], "yr": tk(yrT, c), "ys": tk(ysT, c), "yd": tk(ydT, c),
                         "wbr0": A(w_branch_ret[l]), "wbr1": A(w_branch_sb[l]), "wbr2": A(w_branch_ssd[l]), "wo": A(w_out[l]),
                         "nw": _pc(ssd_norm_w[l], 8), "nmp": _pc(norm_mix_post[l], KC)} for c in range(NCORE)])
        xs = [r1[c]["xo"] for c in range(NCORE)]
        del r1, pgs
        r2 = _run(pC2, [{"xT": xs[c], "wg": A(ffn_w_gate[l]), "wu": A(ffn_w_up[l]), "wd": A(ffn_w_down[l]),
                         "nfp": _pc(norm_ffn_pre[l], KC), "nfo": _pc(norm_ffn_post[l], KC)} for c in range(NCORE)])
        xs = [r2[c]["xo"] for c in range(NCORE)]
        del r2
    out = np.concatenate(xs, axis=1).T
    return np.ascontiguousarray(out[None]).astype(np.float32)
```

```python
import contextlib, math
import numpy as np
from concourse.bass_utils import run_bass_kernel_spmd
import numpy as np
import concourse.bass as bass
import concourse.mybir as mybir

F32 = mybir.dt.float32
BF16 = mybir.dt.bfloat16
I32 = mybir.dt.int32
AF = mybir.ActivationFunctionType
ALU = mybir.AluOpType
AX = mybir.AxisListType


class Serial:
    ENGS = ("sync", "scalar", "tensor", "vector", "gpsimd")

    def __init__(self, nc, tag=""):
        self.nc = nc
        self.tag = tag
        self.ops = []

    @staticmethod
    def _aps(*xs):
        return [x for x in xs if hasattr(x, "tensor")]

    def op(self, eng, fn, dma=False, writes=(), reads=()):
        self.ops.append((eng, fn, dma, list(writes), list(reads)))

    def dma(self, out, in_, eng="sync"):
        self.op(eng, lambda e: e.dma_start(out=out, in_=in_), True, [out], [in_])

    def dma_cast(self, out, in_):
        self.op("gpsimd", lambda e: e.dma_start(out=out, in_=in_), True, [out], [in_])

    def mm(self, out, lhsT, rhs, start=True, stop=True):
        self.op("tensor", lambda e: e.matmul(out, lhsT, rhs, start=start, stop=stop), False, [out], [lhsT, rhs])

    def act(self, out, in_, func, bias=None, scale=None):
        kw = {}
        if bias is not None:
            kw["bias"] = bias
        if scale is not None:
            kw["scale"] = scale
        self.op("scalar", lambda e: e.activation(out=out, in_=in_, func=func, **kw), False, [out], self._aps(in_, bias, scale))

    def vec(self, name, *a, eng="vector", **kw):
        self.op(eng, lambda e: getattr(e, name)(*a, **kw), False, [a[0]], self._aps(*a[1:], *kw.values()))

    def emit(self):
        nc = self.nc
        is_dram = lambda ap: type(ap.tensor).__name__.startswith("DRam")
        nm = lambda ap: ap.tensor.name
        eng_cnt = {e: 0 for e in self.ENGS}
        dma_cnt = {}
        last_w = {}
        rd = {}
        waited = {e: {} for e in self.ENGS}
        plan = {e: [] for e in self.ENGS}
        for eng, fn, is_dma, writes, reads in self.ops:
            deps = {}

            def need(ev):
                if ev is not None and deps.get(ev[0], 0) < ev[1]:
                    deps[ev[0]] = ev[1]
            for r in reads:
                if not is_dram(r):
                    need(last_w.get(nm(r)))
            for w in writes:
                if not is_dram(w):
                    need(last_w.get(nm(w)))
                    for s_, v_ in rd.get(nm(w), {}).items():
                        need((s_, v_))
            waits = []
            for s_, v_ in deps.items():
                if eng == "tensor" and s_ == ("E", "tensor"):
                    continue
                if waited[eng].get(s_, 0) < v_:
                    waited[eng][s_] = v_
                    waits.append((s_, v_))
            if is_dma:
                w0 = writes[0]
                key = ("D", nm(w0)) if not is_dram(w0) else ("D", "src_" + nm(reads[0]))
                dma_cnt[key] = dma_cnt.get(key, 0) + 16
                ev = (key, dma_cnt[key]); inc = 16
            else:
                eng_cnt[eng] += 1
                ev = (("E", eng), eng_cnt[eng]); inc = 1
            plan[eng].append((waits, fn, (ev[0], inc)))
            for r in reads:
                if not is_dram(r):
                    d = rd.setdefault(nm(r), {})
                    d[ev[0]] = max(d.get(ev[0], 0), ev[1])
            for w in writes:
                if not is_dram(w):
                    last_w[nm(w)] = ev
                    rd[nm(w)] = {}
        finals = [(("E", e), c) for e, c in eng_cnt.items() if c] + list(dma_cnt.items())
        import contextlib as _cl
        with _cl.ExitStack() as es:
            keys = [("E", e) for e in self.ENGS] + list(dma_cnt.keys())
            sems = {k: es.enter_context(nc.semaphore("%ss%d_%s" % (self.tag, i, str(k[1])[:20]))) for i, k in enumerate(keys)}
            block = es.enter_context(nc.Block())

            def mk(name):
                lst = plan[name]

                def body(e):
                    for waits, fn, (skey, inc) in lst:
                        for s_, v_ in waits:
                            e.wait_ge(sems[s_], v_)
                        fn(e).then_inc(sems[skey], inc)
                    for s_, v_ in finals:
                        e.wait_ge(sems[s_], v_)
                return body
            block.sync(mk("sync"))
            block.scalar(mk("scalar"))
            block.tensor(mk("tensor"))
            block.vector(mk("vector"))
            block.gpsimd(mk("gpsimd"))
        return len(self.ops)


D = 2048
KC = D // 128
EPS = 1e-6


WT = 256


def pretile(w, wt=WT):
    w = np.asarray(w, np.float32)
    K_, N = w.shape
    nt = -(-N // wt)
    if nt * wt != N:
        w = np.concatenate([w, np.zeros((K_, nt * wt - N), np.float32)], axis=1)
    return np.ascontiguousarray(w.reshape(K_ // 128, 128, nt, wt).transpose(2, 1, 0, 3)).reshape(nt, 128, (K_ // 128) * wt)


def _wload(S_, wst, wtb, i, w_tiled, t, kch, width=WT, c0=0, nch=None):
    nch = kch if nch is None else nch
    n = nch * width
    st = wst[i % len(wst)]; wb = wtb[i % len(wtb)]
    S_.dma(st[:, 0:n], w_tiled[t, :, c0 * width:c0 * width + n])
    S_.vec("tensor_copy", wb[:, 0:n], st[:, 0:n])
    return wb[:, 0:n].rearrange("p (c n) -> p c n", n=width)


def stage_A(nc, tag, T, NM, NG, xT, gain, wm, wg, bg, pm, pg):
    ntm = -(-NM // WT); ntg = -(-NG // WT)
    NH = T // 512
    with contextlib.ExitStack() as es:
        sb = lambda n, s, d: es.enter_context(nc.sbuf_tensor(n, s, d))
        ps = lambda n, s, d: es.enter_context(nc.psum_tensor(n, s, d))
        x32 = sb("x32", [128, KC, T], F32)
        sq = sb("sq", [128, KC, 512], BF16)
        hT = sb("hT", [128, KC, T], BF16)
        g_sb = sb("g_sb", [128, KC], F32)
        bg_sb = sb("bg_sb", [128, NG // 128], F32)
        ones = sb("ones", [128, 128], BF16)
        rstd = sb("rstd", [128, T], F32)
        wst = [sb(f"wst{i}", [128, KC * WT], F32) for i in range(2)]
        wtb = [sb(f"wtb{i}", [128, KC * WT], BF16) for i in range(2)]
        o32 = [sb(f"o32_{i}", [128, 512], F32) for i in range(3)]
        acc = [ps(f"acc{i}", [128, 512], F32) for i in range(3)]
        accn = ps("accn", [128, 512], F32)
        S_ = Serial(nc, tag)
        S_.dma(x32[:], xT.rearrange("(c p) t -> p c t", p=128))
        S_.dma(g_sb[:], gain); S_.dma(bg_sb[:], bg)
        S_.vec("memset", ones[:], 1.0)
        for h in range(NH):
            ts = slice(h * 512, (h + 1) * 512)
            S_.act(sq[:], x32[:, :, ts], AF.Square)
            for c in range(KC):
                S_.mm(accn[:], ones[:], sq[:, c, :], start=(c == 0), stop=(c == KC - 1))
            S_.vec("tensor_scalar", rstd[:, ts], accn[:], 1.0 / D, EPS, ALU.mult, ALU.add)
        S_.act(rstd[:], rstd[:], AF.Sqrt)
        S_.vec("reciprocal", rstd[:], rstd[:])
        for c in range(KC):
            S_.vec("scalar_tensor_tensor", hT[:, c, :], x32[:, c, :], g_sb[:, c:c + 1], rstd[:], ALU.mult, ALU.mult)
        it = 0; ib = 0
        for (w, out, N, gate, nt) in ((wm, pm, NM, False, ntm), (wg, pg, NG, True, ntg)):
            for t in range(nt):
                wv = _wload(S_, wst, wtb, it, w, t, KC); it += 1
                for jb in range(WT // 128):
                    j0 = t * WT + jb * 128
                    wj = min(128, N - j0)
                    if wj <= 0:
                        continue
                    for h in range(NH):
                        ts = slice(h * 512, (h + 1) * 512)
                        a = acc[ib % 3]; o = o32[ib % 3]; ib += 1
                        for c in range(KC):
                            S_.mm(a[0:wj, :], wv[:, c, jb * 128:jb * 128 + wj], hT[:, c, ts], start=(c == 0), stop=(c == KC - 1))
                        if gate:
                            S_.act(o[0:wj, :], a[0:wj, :], AF.Sigmoid, bias=bg_sb[0:wj, j0 // 128:j0 // 128 + 1])
                        else:
                            S_.act(o[0:wj, :], a[0:wj, :], AF.Copy)
                        S_.dma(out[j0:j0 + wj, ts], o[0:wj, :], eng="gpsimd")
        S_.emit()


def build_A(T, NM, NG):
    nc = bass.Bass("TRN2", target_bir_lowering=False)
    ntm = -(-NM // WT); ntg = -(-NG // WT)
    xT = nc.dram_tensor("xT", [D, T], F32, kind="ExternalInput").ap()
    gain = nc.dram_tensor("gain", [128, KC], F32, kind="ExternalInput").ap()
    wm = nc.dram_tensor("wm", [ntm, 128, KC * WT], F32, kind="ExternalInput").ap()
    wg = nc.dram_tensor("wg", [ntg, 128, KC * WT], F32, kind="ExternalInput").ap()
    bg = nc.dram_tensor("bg", [128, NG // 128], F32, kind="ExternalInput").ap()
    pm = nc.dram_tensor("pm", [NM, T], F32, kind="ExternalOutput").ap()
    pg = nc.dram_tensor("pg", [NG, T], F32, kind="ExternalOutput").ap()
    NH = T // 512
    stage_A(nc, "a", T, NM, NG, xT, gain, wm, wg, bg, pm, pg)
    return nc


def sb_consts():
    j = np.arange(128)
    L = (j[:, None] >= j[None, :]).astype(np.float32)
    tl = np.arange(512)
    M = np.stack([(128 * r + j[:, None] < tl[None, :]).astype(np.float32) for r in range(4)], 1)
    return L, M


def emit_sb(nc, S_, es, qT, kT, v, Lc, Mc, yT, SEQ):
    sb = lambda n, s, d: es.enter_context(nc.sbuf_tensor(n, s, d))
    ps = lambda n, s, d: es.enter_context(nc.psum_tensor(n, s, d))
    NB = SEQ // 128
    NG = SEQ // 512
    scale = 128 ** -0.5
    st32 = sb("sb_st32", [128, 2048], F32)
    qb = sb("sb_qb", [128, SEQ], BF16)
    qnb = sb("sb_qnb", [128, SEQ], BF16)
    kb = sb("sb_kb", [128, SEQ], BF16)
    vb = sb("sb_vb", [128, NB, 128], BF16)
    Lb = sb("sb_L", [128, 128], BF16)
    onesb = sb("sb_ones", [128, 128], BF16)
    Mb = sb("sb_M", [128, 4, 512], BF16)
    carry = sb("sb_carry", [128, 512], F32)
    e32 = [sb(f"sb_e{i}", [128, 512], F32) for i in range(2)]
    spb = [sb(f"sb_sp{i}", [128, 512], BF16) for i in range(3)]
    tmp = [sb(f"sb_tmp{i}", [128, 512], F32) for i in range(2)]
    wb = [sb(f"sb_w{i}", [128, 512], BF16) for i in range(3)]
    o32 = [sb(f"sb_o{i}", [128, 512], F32) for i in range(2)]
    Z = [ps(f"sb_Z{i}", [128, 512], F32) for i in range(2)]
    C = [ps(f"sb_C{i}", [128, 512], F32) for i in range(2)]
    Tt = [ps(f"sb_T{i}", [128, 512], F32) for i in range(2)]
    O = [ps(f"sb_O{i}", [128, 512], F32) for i in range(2)]
    for t0 in range(0, SEQ, 2048):
        n = min(2048, SEQ - t0)
        S_.dma(st32[:, 0:n], qT[:, t0:t0 + n])
        S_.act(qb[:, t0:t0 + n], st32[:, 0:n], AF.Copy, scale=scale)
        S_.act(qnb[:, t0:t0 + n], st32[:, 0:n], AF.Copy, scale=-scale)
    S_.dma_cast(kb[:], kT)
    S_.dma_cast(vb[:], v.rearrange("(b p) d -> p b d", p=128))
    S_.dma_cast(Lb[:], Lc)
    S_.dma_cast(Mb[:], Mc)
    S_.vec("memset", onesb[:], 1.0)
    tiles = [(G, kbi) for G in range(NG) for kbi in range(4 * G + 3, -1, -1)]
    NT = len(tiles)

    def st1(i):
        G, kbi = tiles[i]
        qs = slice(G * 512, (G + 1) * 512); ks = slice(kbi * 128, (kbi + 1) * 128)
        S_.mm(Z[i % 2][:], kb[:, ks], qb[:, qs])
        S_.act(e32[i % 2][:], Z[i % 2][:], AF.Exp)
        S_.act(spb[i % 3][:], e32[i % 2][:], AF.Ln, bias=1.0)
        if kbi >= 4 * G:
            S_.vec("tensor_tensor", spb[i % 3][:], spb[i % 3][:], Mb[:, kbi - 4 * G, :], ALU.mult, eng="gpsimd")

    def st2(i):
        G, kbi = tiles[i]
        qs = slice(G * 512, (G + 1) * 512); ks = slice(kbi * 128, (kbi + 1) * 128)
        S_.mm(C[i % 2][:], Lb[:], spb[i % 3][:], start=True, stop=False)
        S_.mm(C[i % 2][:], kb[:, ks], qnb[:, qs], start=False, stop=True)
        S_.mm(Tt[i % 2][:], onesb[:], spb[i % 3][:])
        if kbi == 4 * G + 3:
            S_.vec("tensor_copy", tmp[i % 2][:], C[i % 2][:])
            S_.vec("tensor_copy", carry[:], Tt[i % 2][:])
        else:
            S_.vec("tensor_tensor", tmp[i % 2][:], C[i % 2][:], carry[:], ALU.add)
            S_.vec("tensor_tensor", carry[:], Tt[i % 2][:], carry[:], ALU.add)
        S_.act(wb[i % 3][:], tmp[i % 2][:], AF.Exp, scale=-1.0)
        if kbi >= 4 * G:
            S_.vec("tensor_tensor", wb[i % 3][:], wb[i % 3][:], Mb[:, kbi - 4 * G, :], ALU.mult, eng="gpsimd")

    def st3(i):
        G, kbi = tiles[i]
        S_.mm(O[G % 2][:], vb[:, kbi, :], wb[i % 3][:], start=(kbi == 4 * G + 3), stop=(kbi == 0))
        if kbi == 0:
            S_.vec("tensor_copy", o32[G % 2][:], O[G % 2][:])
            S_.dma(yT[:, G * 512:(G + 1) * 512], o32[G % 2][:])

    for step in range(NT + 2):
        if step < NT:
            st1(step)
        if 0 <= step - 1 < NT:
            st2(step - 1)
        if 0 <= step - 2 < NT:
            st3(step - 2)


def build_sb(SEQ):
    nc = bass.Bass("TRN2", target_bir_lowering=False)
    qT = nc.dram_tensor("qT", [128, SEQ], F32, kind="ExternalInput").ap()
    kT = nc.dram_tensor("kT", [128, SEQ], F32, kind="ExternalInput").ap()
    v = nc.dram_tensor("v", [SEQ, 128], F32, kind="ExternalInput").ap()
    Lc = nc.dram_tensor("Lc", [128, 128], F32, kind="ExternalInput").ap()
    Mc = nc.dram_tensor("Mc", [128, 4, 512], F32, kind="ExternalInput").ap()
    yT = nc.dram_tensor("yT", [128, SEQ], F32, kind="ExternalOutput").ap()
    with contextlib.ExitStack() as es:
        S_ = Serial(nc)
        emit_sb(nc, S_, es, qT, kT, v, Lc, Mc, yT, SEQ)
        S_.emit()
    return nc


def ref_sb(q, k, v):
    s = q.shape[0]
    z = (q.astype(np.float64) @ k.astype(np.float64).T) * 128 ** -0.5
    mask = np.arange(s)[None, :] < np.arange(s)[:, None]
    lk = np.where(mask, -np.logaddexp(0, z), 0.0)
    later = np.cumsum(lk[:, ::-1], 1)[:, ::-1] - lk
    w = np.where(mask, np.exp(-np.logaddexp(0, -z) + later), 0.0)
    return w @ v.astype(np.float64)


EPS = 1e-6


def ret_consts(head):
    lg = np.log1p(-np.exp2(-5.0 - head)).astype(np.float32).astype(np.float64)
    sl = np.arange(128)[:, None]; tl = np.arange(512)[None, :]
    E0 = np.exp(lg * (tl - sl))
    Ed = []
    for r in range(4):
        valid = (2 * r + sl // 64) <= (tl // 64)
        Ed.append(np.where(valid, np.exp(lg * np.abs(tl - 128 * r - sl)), 0.0))
    tab = np.stack([E0] + Ed, 1).astype(np.float32)
    half = 64
    invf = (10000.0 ** (-2.0 * np.arange(half, dtype=np.float32) / 128)).astype(np.float32)
    invf = np.broadcast_to(invf[None, :], (128, half)).copy()
    ident = np.eye(128, dtype=np.float32)
    sc = np.broadcast_to(np.exp(lg * 128.0 * np.arange(64))[None, :], (128, 64)).astype(np.float32).copy()
    return tab, invf, ident, sc


def emit_ret(nc, S_, es, q, k, v, g, pos, gnw, tab, invf, ident, sc, y, SEQ):
    sb = lambda n, s, d: es.enter_context(nc.sbuf_tensor(n, s, d))
    ps = lambda n, s, d: es.enter_context(nc.psum_tensor(n, s, d))
    NB = SEQ // 128; NG = SEQ // 512
    NH_ = 2 if NB >= 8 else 1
    HB = NB // NH_
    qa = sb("rt_qa", [128, HB, 128], F32)
    cos = sb("rt_cos", [128, NB, 64], F32)
    sin = sb("rt_sin", [128, NB, 64], F32)
    t1 = sb("rt_t1", [128, NB, 64], F32)
    t2 = sb("rt_t2", [128, NB, 64], F32)
    rot = sb("rt_rot", [128, HB, 128], F32)
    rotb = sb("rt_rotb", [128, HB, 128], BF16)
    qTb = sb("rt_qTb", [128, SEQ], BF16)
    kTb = sb("rt_kTb", [128, SEQ], BF16)
    vb = sb("rt_vb", [128, NB, 128], BF16)
    posi = sb("rt_posi", [128, NB], I32)
    posf = sb("rt_posf", [128, NB], F32)
    invf_sb = sb("rt_invf", [128, 64], F32)
    identb = sb("rt_ident", [128, 128], BF16)
    tab_sb = sb("rt_tab", [128, 5, 512], F32)
    gnw_sb = sb("rt_gnw", [128, 128], F32)
    sc_sb = sb("rt_sc", [128, 64], F32)
    wt = [sb(f"rt_wt{i}", [128, 512], BF16) for i in range(3)]
    osb = [sb(f"rt_osb{j}", [128, 128], F32) for j in range(4)]
    g_sb = sb("rt_g", [128, 4, 128], F32)
    cen = sb("rt_cen", [128, 128], F32)
    sq = sb("rt_sq", [128, 128], F32)
    st = sb("rt_st", [128, 4], F32)
    yo = sb("rt_yo", [128, 4, 128], F32)
    TP = ps("rt_TP", [128, 512], F32)
    Sc = [ps(f"rt_Sc{i}", [128, 512], F32) for i in range(2)]
    Ob = [ps(f"rt_O{j}", [128, 512], F32) for j in range(4)]
    S_.dma(posi[:], pos); S_.dma(invf_sb[:], invf); S_.dma(tab_sb[:], tab); S_.dma(gnw_sb[:], gnw); S_.dma(sc_sb[:], sc)
    S_.dma_cast(identb[:], ident)
    S_.dma_cast(vb[:], v.rearrange("(b p) d -> p b d", p=128))
    S_.vec("tensor_copy", posf[:], posi[:])
    for b in range(NB):
        S_.vec("tensor_scalar", t1[:, b, :], invf_sb[:], posf[:, b:b + 1], None, ALU.mult)
    MAGIC = 12582912.0
    for (dst, shift) in ((sin, 0.0), (cos, 0.5 * math.pi)):
        if shift:
            S_.vec("tensor_scalar", t1[:], t1[:], shift, None, ALU.add)
        S_.vec("tensor_scalar", t2[:], t1[:], 1.0 / (2 * math.pi), MAGIC, ALU.mult, ALU.add)
        S_.vec("tensor_scalar", t2[:], t2[:], -MAGIC, None, ALU.add)
        S_.vec("scalar_tensor_tensor", t2[:], t2[:], -2 * math.pi, t1[:], ALU.mult, ALU.add)
        S_.vec("tensor_scalar", t2[:], t2[:], math.pi, -math.pi, ALU.min, ALU.max)
        S_.act(dst[:], t2[:], AF.Sin)
    for (src, dstT, scl) in ((q, qTb, 1.0), (k, kTb, 128 ** -0.5)):
        for hb_ in range(NH_):
            b0 = hb_ * HB
            bs = slice(b0, b0 + HB)
            S_.dma(qa[:], src[b0 * 128:(b0 + HB) * 128, :].rearrange("(b p) d -> p b d", p=128))
            a1 = qa[:, :, 0:64]; a2 = qa[:, :, 64:128]
            u1 = t1[:, 0:HB, :]; u2 = t2[:, 0:HB, :]
            S_.vec("tensor_tensor", u1, a1, cos[:, bs, :], ALU.mult)
            S_.vec("tensor_tensor", u2, a2, sin[:, bs, :], ALU.mult)
            S_.vec("tensor_tensor", rot[:, :, 0:64], u1, u2, ALU.subtract)
            S_.vec("tensor_tensor", u1, a1, sin[:, bs, :], ALU.mult)
            S_.vec("tensor_tensor", u2, a2, cos[:, bs, :], ALU.mult)
            S_.vec("tensor_tensor", rot[:, :, 64:128], u1, u2, ALU.add)
            S_.act(rotb[:], rot[:], AF.Copy, scale=scl)
            for G in range(HB // 4):
                Gg = b0 // 4 + G
                for j in range(4):
                    S_.mm(TP[:, j * 128:(j + 1) * 128], rotb[:, 4 * G + j, :], identb[:])
                S_.act(dstT[:, Gg * 512:(Gg + 1) * 512], TP[:], AF.Copy)
    tiles = [(G, kbi) for G in range(NG) for kbi in range(0, 4 * G + 4)]
    NT = len(tiles)

    def st1(i):
        G, kbi = tiles[i]
        S_.mm(Sc[i % 2][:], kTb[:, kbi * 128:(kbi + 1) * 128], qTb[:, G * 512:(G + 1) * 512])
        if kbi >= 4 * G:
            S_.vec("tensor_tensor", wt[i % 3][:], Sc[i % 2][:], tab_sb[:, 1 + kbi - 4 * G, :], ALU.mult)
        else:
            dlt = 4 * G - kbi
            S_.vec("scalar_tensor_tensor", wt[i % 3][:], Sc[i % 2][:], sc_sb[:, dlt:dlt + 1], tab_sb[:, 0, :], ALU.mult, ALU.mult)

    def st2(i):
        G, kbi = tiles[i]
        last = 4 * G + 3
        for j in range(4):
            S_.mm(Ob[j][:, 0:128], wt[i % 3][:, j * 128:(j + 1) * 128], vb[:, kbi, :], start=(kbi == 0), stop=(kbi == last))
        if kbi != last:
            return
        for j in range(4):
            S_.act(osb[j][:], Ob[j][:, 0:128], AF.Copy)
        S_.dma(g_sb[:], g[G * 512:(G + 1) * 512, :].rearrange("(b p) d -> p b d", p=128))
        S_.act(g_sb[:], g_sb[:], AF.Silu)
        for j in range(4):
            S_.vec("reduce_sum", st[:, 0:1], osb[j][:], AX.X)
            S_.vec("tensor_scalar", st[:, 0:1], st[:, 0:1], 1.0 / 128, None, ALU.mult)
            S_.vec("tensor_scalar", cen[:], osb[j][:], st[:, 0:1], None, ALU.subtract)
            S_.vec("tensor_tensor", sq[:], cen[:], cen[:], ALU.mult)
            S_.vec("reduce_sum", st[:, 1:2], sq[:], AX.X)
            S_.vec("tensor_scalar", st[:, 1:2], st[:, 1:2], 1.0 / 128, EPS, ALU.mult, ALU.add)
            S_.act(st[:, 2:3], st[:, 1:2], AF.Sqrt)
            S_.vec("reciprocal", st[:, 3:4], st[:, 2:3])
            S_.vec("scalar_tensor_tensor", cen[:], cen[:], st[:, 3:4], gnw_sb[:], ALU.mult, ALU.mult)
            S_.vec("tensor_tensor", yo[:, j, :], cen[:], g_sb[:, j, :], ALU.mult)
        S_.dma(y[G * 512:(G + 1) * 512, :].rearrange("(b p) d -> p b d", p=128), yo[:])

    for step in range(NT + 1):
        if step < NT:
            st1(step)
        if step >= 1:
            st2(step - 1)


def build_ret(SEQ):
    nc = bass.Bass("TRN2", target_bir_lowering=False)
    NB = SEQ // 128
    di = lambda n, s, d=F32: nc.dram_tensor(n, s, d, kind="ExternalInput").ap()
    q = di("q", [SEQ, 128]); k = di("k", [SEQ, 128]); v = di("v", [SEQ, 128]); g = di("g", [SEQ, 128])
    pos = di("pos", [128, NB], I32); gnw = di("gnw", [128, 128]); tab = di("tab", [128, 5, 512])
    invf = di("invf", [128, 64]); ident = di("ident", [128, 128]); sc = di("sc", [128, 64])
    y = nc.dram_tensor("y", [SEQ, 128], F32, kind="ExternalOutput").ap()
    with contextlib.ExitStack() as es:
        S_ = Serial(nc)
        emit_ret(nc, S_, es, q, k, v, g, pos, gnw, tab, invf, ident, sc, y, SEQ)
        S_.emit()
    return nc


def ssd_consts():
    j = np.arange(128)
    tri = (j[:, None] <= j[None, :]).astype(np.float32)
    tl = np.arange(512)
    Mle = np.stack([(128 * r + j[:, None] <= tl[None, :]).astype(np.float32) for r in range(4)], 1)
    return tri, Mle, np.eye(128, dtype=np.float32)


def emit_ssd(nc, S_, es, xT, BT, CT, zT, dtr, cwx, cbx, cwB, cbB, cwC, cbC, dtb, alog, dsk, tri, Mle, ident, yT, SEQ):
    sb = lambda n, s, d: es.enter_context(nc.sbuf_tensor(n, s, d))
    ps = lambda n, s, d: es.enter_context(nc.psum_tensor(n, s, d))
    NB = SEQ // 128; NG = SEQ // 512
    SL = min(SEQ, 4096)
    up = sb("sd_up", [128, 3 + SL], F32)
    acc = sb("sd_acc", [128, SL], F32)
    xc = [sb(f"sd_xc{e}", [64, SEQ], F32) for e in range(2)]
    Bc = sb("sd_Bc", [128, SEQ], BF16)
    Cc = sb("sd_Cc", [128, SEQ], BF16)
    Btok = sb("sd_Btok", [128, NB, 128], BF16)
    cw = sb("sd_cw", [128, 4], F32); cb = sb("sd_cb", [128, 1], F32)
    dt = sb("sd_dt", [128, NB, 2], F32); ev = sb("sd_ev", [128, NB, 2], F32)
    a = sb("sd_a", [128, NB, 2], F32); acl = sb("sd_acl", [128, NB, 2], F32)
    dfac = sb("sd_dfac", [128, NB, 2], F32); extot = sb("sd_extot", [128, NB, 2], F32)
    dtb_sb = sb("sd_dtb", [128, 2], F32); negA = sb("sd_negA", [128, 2], F32)
    dsk_sb = [sb(f"sd_dsk{e}", [64, 1], F32) for e in range(2)]
    tri_sb = sb("sd_tri", [128, 128], F32); ones32 = sb("sd_ones", [128, 128], F32); id32 = sb("sd_id", [128, 128], F32)
    idb = sb("sd_idb", [128, 128], BF16)
    M0 = sb("sd_M0", [128, 128], F32)
    xdt = sb("sd_xdt", [128, NB, 128], BF16)
    xdtd = sb("sd_xdtd", [128, NB, 128], BF16)
    dg = [sb(f"sd_dg{i}", [128, 128], F32) for i in range(4)]
    seg = [sb(f"sd_seg{i}", [128, 128], F32) for i in range(4)]
    exr = [sb(f"sd_exr{i}", [128, 128], F32) for i in range(4)]
    wt = [sb(f"sd_wt{i}", [128, 128], BF16) for i in range(4)]
    cd = [sb(f"sd_cd{i}", [128, 128], BF16) for i in range(4)]
    St = [sb(f"sd_St{e}", [128, 64], F32) for e in range(2)]
    Sb_ = [sb(f"sd_Sb{e}", [128, 64], BF16) for e in range(2)]
    zt = [sb(f"sd_z{e}", [64, 512], F32) for e in range(2)]; yo = [sb(f"sd_yo{e}", [64, 512], F32) for e in range(2)]
    P1 = ps("sd_P1", [128, 512], F32)
    P2 = ps("sd_P2", [128, 512], F32)
    Pcb = ps("sd_Pcb", [128, 512], F32)
    ARl = [ps(f"sd_ARl{i}", [128, 512], F32) for i in range(2)]
    Y = [ps(f"sd_Y{e}", [64, 512], F32) for e in range(2)]
    for (dst, src) in ((tri_sb, tri), (id32, ident), (M0, Mle[:, 0, 0:128]), (dtb_sb, dtb), (negA, alog)):
        S_.dma(dst[:], src)
    S_.dma_cast(idb[:], ident)
    S_.vec("memset", ones32[:], 1.0)

    def conv(src, w, b, np_, lo, dst):
        S_.dma(cw[0:np_, :], w[lo:lo + np_, :]); S_.dma(cb[0:np_, :], b[lo:lo + np_, :])
        for t0 in range(0, SEQ, SL):
            if t0 == 0:
                S_.vec("memset", up[0:np_, 0:3], 0.0)
                S_.dma(up[0:np_, 3:3 + SL], src[lo:lo + np_, 0:SL])
            else:
                S_.dma(up[0:np_, 0:3 + SL], src[lo:lo + np_, t0 - 3:t0 + SL])
            S_.vec("tensor_scalar", acc[0:np_, :], up[0:np_, 0:SL], cw[0:np_, 0:1], None, ALU.mult)
            for k in range(1, 4):
                S_.vec("scalar_tensor_tensor", acc[0:np_, :], up[0:np_, k:k + SL], cw[0:np_, k:k + 1], acc[0:np_, :], ALU.mult, ALU.add)
            S_.act(dst[0:np_, t0:t0 + SL], acc[0:np_, :], AF.Silu, bias=cb[0:np_, :])
    for e in range(2):
        conv(xT, cwx, cbx, 64, 64 * e, xc[e])
        S_.dma(dsk_sb[e][:], dsk[64 * e:64 * e + 64, :])
    conv(BT, cwB, cbB, 128, 0, Bc)
    conv(CT, cwC, cbC, 128, 0, Cc)
    S_.dma(dt[:], dtr)
    for e in range(2):
        S_.vec("tensor_scalar", dt[:, :, e], dt[:, :, e], dtb_sb[:, e:e + 1], None, ALU.add)
    S_.act(ev[:], dt[:], AF.Exp)
    S_.act(dt[:], ev[:], AF.Ln, bias=1.0)
    S_.act(negA[:], negA[:], AF.Exp)
    S_.vec("tensor_scalar", negA[:], negA[:], -1.0, None, ALU.mult)
    for e in range(2):
        S_.vec("tensor_scalar", a[:, :, e], dt[:, :, e], negA[:, e:e + 1], None, ALU.mult)
    fl = lambda t: t[:].rearrange("p b e -> p (b e)")
    S_.mm(P1[:, 0:NB * 2], tri_sb[:], fl(a))
    S_.mm(P2[:, 0:NB * 2], ones32[:], fl(a))
    S_.vec("tensor_copy", fl(acl), P1[:, 0:NB * 2])
    S_.vec("tensor_copy", fl(ev), P2[:, 0:NB * 2])
    S_.vec("tensor_tensor", fl(dfac), fl(ev), fl(acl), ALU.subtract)
    S_.act(fl(dfac), fl(dfac), AF.Exp)
    S_.act(fl(extot), fl(ev), AF.Exp)
    for b in range(NB):
        for e in range(2):
            S_.mm(P1[:, 64 * e:64 * e + 64], xc[e][:, b * 128:(b + 1) * 128], id32[0:64, 0:64])
        S_.mm(P2[:, 0:128], Bc[:, b * 128:(b + 1) * 128], idb[:])
        for e in range(2):
            S_.vec("tensor_scalar", xdt[:, b, 64 * e:64 * e + 64], P1[:, 64 * e:64 * e + 64], dt[:, b, e:e + 1], None, ALU.mult)
            S_.vec("tensor_scalar", xdtd[:, b, 64 * e:64 * e + 64], P1[:, 64 * e:64 * e + 64], dt[:, b, e:e + 1], dfac[:, b, e:e + 1], ALU.mult, ALU.mult)
        S_.act(Btok[:, b, :], P2[:, 0:128], AF.Copy)
    for e in range(2):
        S_.vec("memset", St[e][:], 0.0)
        S_.vec("memset", Sb_[e][:], 0.0)
    def pre(b):
        bs = slice(b * 128, (b + 1) * 128)
        pc = Pcb[:, (b % 2) * 128:(b % 2) * 128 + 128]
        ix = [2 * (b % 2) + e for e in range(2)]
        ar = [ARl[b % 2][:, e * 128:(e + 1) * 128] for e in range(2)]
        S_.mm(pc, Bc[:, bs], Cc[:, bs])
        for e in range(2):
            S_.vec("tensor_scalar", dg[ix[e]][:], id32[:], acl[:, b, e:e + 1], None, ALU.mult)
        for e in range(2):
            S_.mm(ar[e], ones32[:], dg[ix[e]][:])
        for e in range(2):
            S_.vec("tensor_scalar", seg[ix[e]][:], ar[e], acl[:, b, e:e + 1], 0.0, ALU.subtract, ALU.min)
        for e in range(2):
            S_.act(seg[ix[e]][:], seg[ix[e]][:], AF.Exp)
            S_.act(exr[ix[e]][:], ar[e], AF.Exp)
        for e in range(2):
            S_.vec("tensor_tensor", seg[ix[e]][:], seg[ix[e]][:], M0[:], ALU.mult)
        for e in range(2):
            S_.vec("tensor_tensor", wt[ix[e]][:], pc, seg[ix[e]][:], ALU.mult)
            S_.vec("tensor_tensor", cd[ix[e]][:], Cc[:, bs], exr[ix[e]][:], ALU.mult)

    def post(b):
        G, j = divmod(b, 4)
        ix = [2 * (b % 2) + e for e in range(2)]
        for e in range(2):
            yv = Y[e][:, j * 128:(j + 1) * 128]
            S_.mm(yv, xdt[:, b, 64 * e:64 * e + 64], wt[ix[e]][:], start=True, stop=False)
            S_.mm(yv, Sb_[e][:], cd[ix[e]][:], start=False, stop=True)
            S_.mm(P1[:, 64 * e:64 * e + 64], Btok[:, b, :], xdtd[:, b, 64 * e:64 * e + 64])
        for e in range(2):
            S_.vec("scalar_tensor_tensor", St[e][:], St[e][:], extot[:, b, e:e + 1], P1[:, 64 * e:64 * e + 64], ALU.mult, ALU.add)
        for e in range(2):
            S_.act(Sb_[e][:], St[e][:], AF.Copy)
        if j == 3:
            qs = slice(G * 512, (G + 1) * 512)
            for e in range(2):
                S_.dma(zt[e][:], zT[64 * e:64 * e + 64, qs])
                S_.act(zt[e][:], zt[e][:], AF.Silu)
                S_.vec("scalar_tensor_tensor", yo[e][:], xc[e][:, qs], dsk_sb[e][:, 0:1], Y[e][:], ALU.mult, ALU.add)
                S_.vec("tensor_tensor", yo[e][:], yo[e][:], zt[e][:], ALU.mult)
                S_.dma(yT[64 * e:64 * e + 64, qs], yo[e][:])

    pre(0)
    for b in range(NB):
        if b + 1 < NB:
            pre(b + 1)
        post(b)


def build_ssd(SEQ):
    nc = bass.Bass("TRN2", target_bir_lowering=False)
    NB = SEQ // 128
    di = lambda n, s, d=F32: nc.dram_tensor(n, s, d, kind="ExternalInput").ap()
    args = [di("xT", [128, SEQ]), di("BT", [128, SEQ]), di("CT", [128, SEQ]), di("zT", [128, SEQ]), di("dtr", [128, NB, 2]),
            di("cwx", [128, 4]), di("cbx", [128, 1]), di("cwB", [128, 4]), di("cbB", [128, 1]), di("cwC", [128, 4]), di("cbC", [128, 1]),
            di("dtb", [128, 2]), di("alog", [128, 2]), di("dsk", [128, 1]), di("tri", [128, 128]), di("Mle", [128, 4, 512]), di("ident", [128, 128])]
    yT = nc.dram_tensor("yT", [128, SEQ], F32, kind="ExternalOutput").ap()
    with contextlib.ExitStack() as es:
        S_ = Serial(nc)
        emit_ssd(nc, S_, es, *args, yT, SEQ)
        S_.emit()
    return nc


def ssd_inputs(c, z, xbc, dtraw, conv_w, conv_b, dt_bias, a_log, d_skip, SEQ):
    g = c // 2
    ch = slice(128 * c, 128 * c + 128); Bs = slice(1024 + 128 * g, 1024 + 128 * g + 128); Cs = slice(1536 + 128 * g, 1536 + 128 * g + 128)
    T = lambda a: np.ascontiguousarray(a.T)
    rep = lambda v: np.broadcast_to(v[None, :], (128, v.shape[0])).copy()
    tri, Mle, ident = ssd_consts()
    return {"xT": T(xbc[:, ch]), "BT": T(xbc[:, Bs]), "CT": T(xbc[:, Cs]), "zT": T(z[:, ch]),
            "dtr": np.ascontiguousarray(dtraw[:, 2 * c:2 * c + 2].reshape(SEQ // 128, 128, 2).transpose(1, 0, 2)),
            "cwx": T(conv_w[:, ch]), "cbx": conv_b[ch].reshape(128, 1).copy(), "cwB": T(conv_w[:, Bs]), "cbB": conv_b[Bs].reshape(128, 1).copy(),
            "cwC": T(conv_w[:, Cs]), "cbC": conv_b[Cs].reshape(128, 1).copy(),
            "dtb": rep(dt_bias[2 * c:2 * c + 2]), "alog": rep(a_log[2 * c:2 * c + 2]),
            "dsk": np.repeat(d_skip[2 * c:2 * c + 2], 64).reshape(128, 1).copy(), "tri": tri, "Mle": Mle, "ident": ident}


def _rstd_from(S_, ones, acc, sqt, nch, dn, rstd):
    for c in range(nch):
        S_.mm(acc[:], ones[:], sqt[:, c, :], start=(c == 0), stop=(c == nch - 1))
    S_.vec("tensor_scalar", rstd[:], acc[:], 1.0 / dn, 1e-6, ALU.mult, ALU.add)
    S_.act(rstd[:], rstd[:], AF.Sqrt)
    S_.vec("reciprocal", rstd[:], rstd[:])


def stage_C1(nc, tag, T, xT, pg, ys, wbr, wo, nw, nmp, xo):
    with contextlib.ExitStack() as es:
        sb = lambda n, s, d: es.enter_context(nc.sbuf_tensor(n, s, d))
        ps = lambda n, s, d: es.enter_context(nc.psum_tensor(n, s, d))
        x32 = sb("c1_x", [128, KC, 512], F32)
        yb = [sb(f"c1_y{i}", [128, 8, 512], BF16) for i in range(3)]
        sq = sb("c1_sq", [128, KC, 512], BF16)
        mg = sb("c1_mg", [128, KC, 512], BF16)
        m32 = [sb(f"c1_m32_{i}", [128, 512], F32) for i in range(2)]
        tmp = [sb(f"c1_tmp{i}", [128, 512], F32) for i in range(2)]
        gt = [sb(f"c1_gt{i}", [128, 512], F32) for i in range(3)]
        o32 = sb("c1_o", [128, KC, 512], F32)
        wst = [sb(f"c1_wst{i}", [128, KC * WT], F32) for i in range(2)]
        wtb = [sb(f"c1_wtb{i}", [128, KC * WT], BF16) for i in range(2)]
        ones = sb("c1_ones", [128, 128], BF16)
        rstd = sb("c1_rstd", [128, 512], F32)
        nw_sb = sb("c1_nw", [128, 8], F32); nmp_sb = sb("c1_nmp", [128, KC], F32)
        acc = [ps(f"c1_acc{i}", [128, 512], F32) for i in range(3)]
        acc2 = ps("c1_accn", [128, 512], F32)
        S_ = Serial(nc, tag)
        S_.vec("memset", ones[:], 1.0)
        S_.dma(nw_sb[:], nw); S_.dma(nmp_sb[:], nmp)
        it = 0; ib = 0; ig = 0
        for h in range(T // 512):
            ts = slice(h * 512, (h + 1) * 512)
            S_.dma(x32[:], xT[:, ts].rearrange("(c p) t -> p c t", p=128))
            for i in range(3):
                S_.dma_cast(yb[i][:], ys[i][:, ts].rearrange("(c p) t -> p c t", p=128))
            S_.act(sq[:, 0:8, :], yb[2][:], AF.Square)
            _rstd_from(S_, ones, acc2, sq, 8, 1024.0, rstd)
            for c in range(8):
                S_.vec("scalar_tensor_tensor", yb[2][:, c, :], yb[2][:, c, :], nw_sb[:, c:c + 1], rstd[:], ALU.mult, ALU.mult)
            for t in range(D // WT):
                wv = []
                for i in range(3):
                    wv.append(_wload(S_, wst, wtb, it, wbr[i], t, 8)); it += 1
                    for jb in range(WT // 128):
                        j = t * (WT // 128) + jb
                        a = acc[ib % 3]; ib += 1
                        for c in range(8):
                            S_.mm(a[:], wv[i][:, c, jb * 128:(jb + 1) * 128], yb[i][:, c, :], start=(c == 0), stop=(c == 7))
                        g_ = gt[ig % 3]; ig += 1
                        S_.dma(g_[:], pg[i * D + j * 128:i * D + (j + 1) * 128, ts], eng="gpsimd")
                        m_ = m32[jb]
                        if i == 0:
                            S_.vec("tensor_tensor", m_[:], a[:], g_[:], ALU.mult)
                        else:
                            S_.vec("tensor_tensor", tmp[jb][:], a[:], g_[:], ALU.mult)
                            S_.vec("tensor_tensor", m_[:], m_[:], tmp[jb][:], ALU.add)
                        if i == 2:
                            S_.act(mg[:, j, :], m_[:], AF.Copy)
            for t in range(D // WT):
                wv_ = _wload(S_, wst, wtb, it, wo, t, KC); it += 1
                for jb in range(WT // 128):
                    j = t * (WT // 128) + jb
                    a = acc[ib % 3]; ib += 1
                    for c in range(KC):
                        S_.mm(a[:], wv_[:, c, jb * 128:(jb + 1) * 128], mg[:, c, :], start=(c == 0), stop=(c == KC - 1))
                    S_.act(o32[:, j, :], a[:], AF.Copy)
            S_.act(sq[:], o32[:], AF.Square)
            _rstd_from(S_, ones, acc2, sq, KC, float(D), rstd)
            for c in range(KC):
                S_.vec("scalar_tensor_tensor", o32[:, c, :], o32[:, c, :], nmp_sb[:, c:c + 1], rstd[:], ALU.mult, ALU.mult)
                S_.vec("tensor_tensor", x32[:, c, :], x32[:, c, :], o32[:, c, :], ALU.add)
            S_.dma(xo[:, ts].rearrange("(c p) t -> p c t", p=128), x32[:])
        S_.emit()


def build_C1(T):
    nc = bass.Bass("TRN2", target_bir_lowering=False)
    di = lambda n, s, d=F32: nc.dram_tensor(n, s, d, kind="ExternalInput").ap()
    xT = di("xT", [D, T]); pg = di("pg", [3 * D, T])
    ys = [di("yr", [1024, T]), di("ys", [1024, T]), di("yd", [1024, T])]
    wbr = [di(f"wbr{i}", [D // WT, 128, 8 * WT]) for i in range(3)]
    wo = di("wo", [D // WT, 128, KC * WT]); nw = di("nw", [128, 8]); nmp = di("nmp", [128, KC])
    xo = nc.dram_tensor("xo", [D, T], F32, kind="ExternalOutput").ap()
    stage_C1(nc, "c1", T, xT, pg, ys, wbr, wo, nw, nmp, xo)
    return nc


FH = 5632
FC = FH // 128


def stage_C2(nc, tag, T, xT, wg, wu, wd, nfp, nfo, xo):
    HK_unused = None
    with contextlib.ExitStack() as es:
        sb = lambda n, s, d: es.enter_context(nc.sbuf_tensor(n, s, d))
        ps = lambda n, s, d: es.enter_context(nc.psum_tensor(n, s, d))
        x32 = sb("c2_x", [128, KC, 512], F32)
        sq = sb("c2_sq", [128, KC, 512], BF16)
        hT = sb("c2_h", [128, KC, 512], BF16)
        aT = sb("c2_a", [128, FC, 512], BF16)
        sg = [sb(f"c2_sg{i}", [128, 512], F32) for i in range(2)]
        o32 = sb("c2_o", [128, KC, 512], F32)
        wst = [sb(f"c2_wst{i}", [128, KC * WT], F32) for i in range(2)]
        wtb = [sb(f"c2_wtb{i}", [128, KC * WT], BF16) for i in range(2)]
        ones = sb("c2_ones", [128, 128], BF16)
        rstd = sb("c2_rstd", [128, 512], F32)
        nfp_sb = sb("c2_nfp", [128, KC], F32); nfo_sb = sb("c2_nfo", [128, KC], F32)
        accg = [ps(f"c2_accg{i}", [128, 512], F32) for i in range(2)]
        accu = [ps(f"c2_accu{i}", [128, 512], F32) for i in range(2)]
        accd = [ps(f"c2_accd{i}", [128, 512], F32) for i in range(2)]
        acc2 = ps("c2_acc2", [128, 512], F32)
        S_ = Serial(nc, tag)
        S_.vec("memset", ones[:], 1.0)
        S_.dma(nfp_sb[:], nfp); S_.dma(nfo_sb[:], nfo)
        it = 0; ib = 0
        HK = FC // 2
        for h in range(T // 512):
            ts = slice(h * 512, (h + 1) * 512)
            S_.dma(x32[:], xT[:, ts].rearrange("(c p) t -> p c t", p=128))
            S_.act(sq[:], x32[:], AF.Square)
            _rstd_from(S_, ones, acc2, sq, KC, float(D), rstd)
            for c in range(KC):
                S_.vec("scalar_tensor_tensor", hT[:, c, :], x32[:, c, :], nfp_sb[:, c:c + 1], rstd[:], ALU.mult, ALU.mult)
            for t in range(FH // WT):
                wgv = _wload(S_, wst, wtb, it, wg, t, KC); it += 1
                ags = []
                for jb in range(WT // 128):
                    a = accg[jb]
                    for c in range(KC):
                        S_.mm(a[:], wgv[:, c, jb * 128:(jb + 1) * 128], hT[:, c, :], start=(c == 0), stop=(c == KC - 1))
                    S_.act(sg[jb][:], a[:], AF.Silu)
                wuv = _wload(S_, wst, wtb, it, wu, t, KC); it += 1
                for jb in range(WT // 128):
                    m = t * (WT // 128) + jb
                    a = accu[jb]
                    for c in range(KC):
                        S_.mm(a[:], wuv[:, c, jb * 128:(jb + 1) * 128], hT[:, c, :], start=(c == 0), stop=(c == KC - 1))
                    S_.vec("tensor_tensor", aT[:, m, :], a[:], sg[jb][:], ALU.mult)
            for j in range(KC):
                a = accd[j % 2]
                for hk in range(2):
                    wdv = _wload(S_, wst, wtb, it, wd, j, FC, width=128, c0=hk * HK, nch=HK); it += 1
                    for m in range(HK):
                        mm_ = hk * HK + m
                        S_.mm(a[:], wdv[:, m, :], aT[:, mm_, :], start=(mm_ == 0), stop=(mm_ == FC - 1))
                S_.act(o32[:, j, :], a[:], AF.Copy)
            S_.act(sq[:], o32[:], AF.Square)
            _rstd_from(S_, ones, acc2, sq, KC, float(D), rstd)
            for c in range(KC):
                S_.vec("scalar_tensor_tensor", o32[:, c, :], o32[:, c, :], nfo_sb[:, c:c + 1], rstd[:], ALU.mult, ALU.mult)
                S_.vec("tensor_tensor", x32[:, c, :], x32[:, c, :], o32[:, c, :], ALU.add)
            S_.dma(xo[:, ts].rearrange("(c p) t -> p c t", p=128), x32[:])
        S_.emit()


def build_C2(T):
    nc = bass.Bass("TRN2", target_bir_lowering=False)
    di = lambda n, s, d=F32: nc.dram_tensor(n, s, d, kind="ExternalInput").ap()
    xT = di("xT", [D, T]); wg = di("wg", [FH // WT, 128, KC * WT]); wu = di("wu", [FH // WT, 128, KC * WT])
    wd = di("wd", [D // 128, 128, FC * 128])
    nfp = di("nfp", [128, KC]); nfo = di("nfo", [128, KC])
    xo = nc.dram_tensor("xo", [D, T], F32, kind="ExternalOutput").ap()
    stage_C2(nc, "c2", T, xT, wg, wu, wd, nfp, nfo, xo)
    return nc


SEQ = 8192
NCORE = 8
TPC = SEQ // NCORE
NMIX = 10256
_PROG = {}


def _prog(name, fn):
    if name not in _PROG:
        _PROG[name] = fn()
    return _PROG[name]


def _run(nc, maps):
    return run_bass_kernel_spmd(nc, maps, core_ids=list(range(NCORE))).results


def _pc(v, n):
    return np.ascontiguousarray(np.asarray(v, np.float32).reshape(n, 128).T)


def build_tail(T, with_A):
    nc = bass.Bass("TRN2", target_bir_lowering=False)
    di = lambda n, s, d=F32: nc.dram_tensor(n, s, d, kind="ExternalInput").ap()
    xT = di("xT", [D, T]); pg = di("pg_in", [3 * D, T])
    ys = [di("yr", [1024, T]), di("ys", [1024, T]), di("yd", [1024, T])]
    wbr = [di(f"wbr{i}", [D // WT, 128, 8 * WT]) for i in range(3)]
    wo = di("wo", [D // WT, 128, KC * WT]); nw = di("nw", [128, 8]); nmp = di("nmp", [128, KC])
    fg = di("fg", [FH // WT, 128, KC * WT]); fu = di("fu", [FH // WT, 128, KC * WT]); fd = di("fd", [D // 128, 128, FC * 128])
    nfp = di("nfp", [128, KC]); nfo = di("nfo", [128, KC])
    xmid = nc.dram_tensor("xmid", [D, T], F32).ap()
    xo = nc.dram_tensor("xo", [D, T], F32, kind="ExternalOutput").ap()
    stage_C1(nc, "c1", T, xT, pg, ys, wbr, wo, nw, nmp, xmid)
    stage_C2(nc, "c2", T, xmid, fg, fu, fd, nfp, nfo, xo)
    if with_A:
        NM, NG = NMIX, 3 * D
        ntm = -(-NM // WT); ntg = -(-NG // WT)
        gain = di("gain", [128, KC]); wm = di("wm", [ntm, 128, KC * WT]); wg = di("wg", [ntg, 128, KC * WT]); bg = di("bg", [128, NG // 128])
        pm = nc.dram_tensor("pm", [NM, T], F32, kind="ExternalOutput").ap()
        pgo = nc.dram_tensor("pg", [NG, T], F32, kind="ExternalOutput").ap()
        stage_A(nc, "a", T, NM, NG, xo, gain, wm, wg, bg, pm, pgo)
    return nc


def build_mix(SEQ):
    nc = bass.Bass("TRN2", target_bir_lowering=False)
    NB = SEQ // 128
    di = lambda n, s, d=F32: nc.dram_tensor(n, s, d, kind="ExternalInput").ap()
    do = lambda n, s: nc.dram_tensor(n, s, F32, kind="ExternalOutput").ap()
    ident = di("ident", [128, 128])
    sbi = [di("i_sb_qT", [128, SEQ]), di("i_sb_kT", [128, SEQ]), di("i_sb_v", [SEQ, 128]), di("i_sb_Lc", [128, 128]), di("i_sb_Mc", [128, 4, 512])]
    sb_y = do("o_sb_yT", [128, SEQ])
    rti = [di("i_rt_q", [SEQ, 128]), di("i_rt_k", [SEQ, 128]), di("i_rt_v", [SEQ, 128]), di("i_rt_g", [SEQ, 128]), di("i_rt_pos", [128, NB], I32),
           di("i_rt_gnw", [128, 128]), di("i_rt_tab", [128, 5, 512]), di("i_rt_invf", [128, 64])]
    rt_sc = di("i_rt_sc", [128, 64]); rt_y = do("o_rt_y", [SEQ, 128])
    sdi = [di("i_sd_xT", [128, SEQ]), di("i_sd_BT", [128, SEQ]), di("i_sd_CT", [128, SEQ]), di("i_sd_zT", [128, SEQ]), di("i_sd_dtr", [128, NB, 2]),
           di("i_sd_cwx", [128, 4]), di("i_sd_cbx", [128, 1]), di("i_sd_cwB", [128, 4]), di("i_sd_cbB", [128, 1]), di("i_sd_cwC", [128, 4]), di("i_sd_cbC", [128, 1]),
           di("i_sd_dtb", [128, 2]), di("i_sd_alog", [128, 2]), di("i_sd_dsk", [128, 1]), di("i_sd_tri", [128, 128]), di("i_sd_Mle", [128, 4, 512])]
    sd_y = do("o_sd_yT", [128, SEQ])
    with contextlib.ExitStack() as es:
        S_ = Serial(nc, "sb")
        emit_sb(nc, S_, es, *sbi, sb_y, SEQ)
        S_.emit()
    with contextlib.ExitStack() as es:
        S_ = Serial(nc, "rt")
        emit_ret(nc, S_, es, *rti, ident, rt_sc, rt_y, SEQ)
        S_.emit()
    with contextlib.ExitStack() as es:
        S_ = Serial(nc, "sd")
        emit_ssd(nc, S_, es, *sdi, ident, sd_y, SEQ)
        S_.emit()
    return nc


def kernel(x, positions, norm_mix_pre, norm_mix_post, norm_ffn_pre, norm_ffn_post, w_in, b_gate,
           ret_gn_w, ssd_conv_w, ssd_conv_b, ssd_dt_bias, ssd_a_log, ssd_d, ssd_norm_w,
           w_branch_ret, w_branch_sb, w_branch_ssd, w_out, ffn_w_gate, ffn_w_up, ffn_w_down):
    A = lambda a: np.asarray(a)
    C = np.ascontiguousarray
    x = A(x).astype(np.float32, copy=False)
    depth = A(w_in).shape[0]
    xT = C(x[0].T)
    xs = [C(xT[:, c * TPC:(c + 1) * TPC]) for c in range(NCORE)]
    pos_l = C(A(positions)[0].astype(np.int32).reshape(SEQ // 128, 128).T)
    Lc, Mc = sb_consts()
    rc = [ret_consts(c) for c in range(NCORE)]
    pA = _prog("A", lambda: build_A(TPC, NMIX, 3 * D))
    pMX = _prog("MIX", lambda: build_mix(SEQ))
    pT1 = _prog("TAILA", lambda: build_tail(TPC, True))
    pT0 = _prog("TAIL", lambda: build_tail(TPC, False))

    def a_inputs(l):
        wl = A(w_in[l])
        return {"gain": _pc(norm_mix_pre[l], KC), "wm": pretile(wl[:, :NMIX]), "wg": pretile(wl[:, NMIX:]), "bg": _pc(b_gate[l], 3 * D // 128)}

    ai = a_inputs(0)
    rA = _run(pA, [{"xT": xs[c], **ai} for c in range(NCORE)])
    pgs = [rA[c]["pg"] for c in range(NCORE)]
    pT = np.concatenate([rA[c]["pm"] for c in range(NCORE)], axis=1)
    del rA, ai
    for l in range(depth):
        hb = lambda base, c: pT[base + 128 * c: base + 128 * (c + 1)]
        gn = A(ret_gn_w[l])
        z_tm = pT[7168:8192].T; xbc_tm = pT[8192:10240].T; dt_tm = pT[10240:10256].T
        maps = []
        for c in range(NCORE):
            sd = ssd_inputs(c, z_tm, xbc_tm, dt_tm, A(ssd_conv_w[l]), A(ssd_conv_b[l]), A(ssd_dt_bias[l]), A(ssd_a_log[l]), A(ssd_d[l]), SEQ)
            m = {"ident": sd.pop("ident")}
            m.update({"i_sd_" + k_: v_ for k_, v_ in sd.items()})
            m.update({"i_sb_qT": C(hb(4096, c)), "i_sb_kT": C(hb(5120, c)), "i_sb_v": C(hb(6144, c).T), "i_sb_Lc": Lc, "i_sb_Mc": Mc})
            m.update({"i_rt_q": C(hb(0, c).T), "i_rt_k": C(hb(1024, c).T), "i_rt_v": C(hb(2048, c).T), "i_rt_g": C(hb(3072, c).T), "i_rt_pos": pos_l,
                      "i_rt_gnw": np.broadcast_to(gn[128 * c:128 * (c + 1)][None, :], (128, 128)).copy(),
                      "i_rt_tab": rc[c][0], "i_rt_invf": rc[c][1], "i_rt_sc": rc[c][3]})
            maps.append(m)
        rM = _run(pMX, maps)
        ysT = np.concatenate([rM[c]["o_sb_yT"] for c in range(NCORE)], axis=0)
        yrT = np.concatenate([rM[c]["o_rt_y"].T for c in range(NCORE)], axis=0)
        ydT = np.concatenate([rM[c]["o_sd_yT"] for c in range(NCORE)], axis=0)
        del pT, rM, maps
        tk = lambda a, c: C(a[:, c * TPC:(c + 1) * TPC])
        shared = {"wbr0": pretile(A(w_branch_ret[l])), "wbr1": pretile(A(w_branch_sb[l])), "wbr2": pretile(A(w_branch_ssd[l])), "wo": pretile(A(w_out[l])),
                  "nw": _pc(ssd_norm_w[l], 8), "nmp": _pc(norm_mix_post[l], KC),
                  "fg": pretile(A(ffn_w_gate[l])), "fu": pretile(A(ffn_w_up[l])), "fd": pretile(A(ffn_w_down[l]), 128),
                  "nfp": _pc(norm_ffn_pre[l], KC), "nfo": _pc(norm_ffn_post[l], KC)}
        last = (l == depth - 1)
        if not last:
            shared.update(a_inputs(l + 1))
        rT = _run(pT0 if last else pT1, [{"xT": xs[c], "pg_in": pgs[c], "yr": tk(yrT, c), "ys": tk(ysT, c), "yd": tk(ydT, c), **shared} for c in range(NCORE)])
        xs = [rT[c]["xo"] for c in range(NCORE)]
        if not last:
            pgs = [rT[c]["pg"] for c in range(NCORE)]
            pT = np.concatenate([rT[c]["pm"] for c in range(NCORE)], axis=1)
        del rT, shared
    out = np.concatenate(xs, axis=1).T
    return np.ascontiguousarray(out[None]).astype(np.float32)
```

```python
import contextlib, math
import numpy as np
from concourse.bass_utils import run_bass_kernel_spmd
import numpy as np
import concourse.bass as bass
import concourse.mybir as mybir

F32 = mybir.dt.float32
BF16 = mybir.dt.bfloat16
I32 = mybir.dt.int32
AF = mybir.ActivationFunctionType
ALU = mybir.AluOpType
AX = mybir.AxisListType


class Serial:
    ENGS = ("sync", "scalar", "tensor", "vector", "gpsimd")

    def __init__(self, nc, tag=""):
        self.nc = nc
        self.tag = tag
        self.ops = []

    @staticmethod
    def _aps(*xs):
        return [x for x in xs if hasattr(x, "tensor")]

    def op(self, eng, fn, dma=False, writes=(), reads=()):
        self.ops.append((eng, fn, dma, list(writes), list(reads)))

    def dma(self, out, in_, eng="sync"):
        self.op(eng, lambda e: e.dma_start(out=out, in_=in_), True, [out], [in_])

    def dma_cast(self, out, in_):
        self.op("gpsimd", lambda e: e.dma_start(out=out, in_=in_), True, [out], [in_])

    def mm(self, out, lhsT, rhs, start=True, stop=True):
        self.op("tensor", lambda e: e.matmul(out, lhsT, rhs, start=start, stop=stop), False, [out], [lhsT, rhs])

    def act(self, out, in_, func, bias=None, scale=None):
        kw = {}
        if bias is not None:
            kw["bias"] = bias
        if scale is not None:
            kw["scale"] = scale
        self.op("scalar", lambda e: e.activation(out=out, in_=in_, func=func, **kw), False, [out], self._aps(in_, bias, scale))

    def vec(self, name, *a, eng="vector", **kw):
        self.op(eng, lambda e: getattr(e, name)(*a, **kw), False, [a[0]], self._aps(*a[1:], *kw.values()))

    def emit(self):
        nc = self.nc
        is_dram = lambda ap: type(ap.tensor).__name__.startswith("DRam")
        nm = lambda ap: ap.tensor.name
        eng_cnt = {e: 0 for e in self.ENGS}
        dma_cnt = {}
        last_w = {}
        rd = {}
        waited = {e: {} for e in self.ENGS}
        plan = {e: [] for e in self.ENGS}
        for eng, fn, is_dma, writes, reads in self.ops:
            deps = {}

            def need(ev):
                if ev is not None and deps.get(ev[0], 0) < ev[1]:
                    deps[ev[0]] = ev[1]
            for r in reads:
                if not is_dram(r):
                    need(last_w.get(nm(r)))
            for w in writes:
                if not is_dram(w):
                    need(last_w.get(nm(w)))
                    for s_, v_ in rd.get(nm(w), {}).items():
                        need((s_, v_))
            waits = []
            for s_, v_ in deps.items():
                if eng == "tensor" and s_ == ("E", "tensor"):
                    continue
                if waited[eng].get(s_, 0) < v_:
                    waited[eng][s_] = v_
                    waits.append((s_, v_))
            if is_dma:
                w0 = writes[0]
                key = ("D", nm(w0)) if not is_dram(w0) else ("D", "src_" + nm(reads[0]))
                dma_cnt[key] = dma_cnt.get(key, 0) + 16
                ev = (key, dma_cnt[key]); inc = 16
            else:
                eng_cnt[eng] += 1
                ev = (("E", eng), eng_cnt[eng]); inc = 1
            plan[eng].append((waits, fn, (ev[0], inc)))
            for r in reads:
                if not is_dram(r):
                    d = rd.setdefault(nm(r), {})
                    d[ev[0]] = max(d.get(ev[0], 0), ev[1])
            for w in writes:
                if not is_dram(w):
                    last_w[nm(w)] = ev
                    rd[nm(w)] = {}
        finals = [(("E", e), c) for e, c in eng_cnt.items() if c] + list(dma_cnt.items())
        import contextlib as _cl
        with _cl.ExitStack() as es:
            keys = [("E", e) for e in self.ENGS] + list(dma_cnt.keys())
            sems = {k: es.enter_context(nc.semaphore("%ss%d_%s" % (self.tag, i, str(k[1])[:20]))) for i, k in enumerate(keys)}
            block = es.enter_context(nc.Block())

            def mk(name):
                lst = plan[name]

                def body(e):
                    for waits, fn, (skey, inc) in lst:
                        for s_, v_ in waits:
                            e.wait_ge(sems[s_], v_)
                        fn(e).then_inc(sems[skey], inc)
                    for s_, v_ in finals:
                        e.wait_ge(sems[s_], v_)
                return body
            block.sync(mk("sync"))
            block.scalar(mk("scalar"))
            block.tensor(mk("tensor"))
            block.vector(mk("vector"))
            block.gpsimd(mk("gpsimd"))
        return len(self.ops)


D = 2048
KC = D // 128
EPS = 1e-6


WT = 256


def pretile(w, wt=WT):
    w = np.asarray(w, np.float32)
    K_, N = w.shape
    nt = -(-N // wt)
    if nt * wt != N:
        w = np.concatenate([w, np.zeros((K_, nt * wt - N), np.float32)], axis=1)
    return np.ascontiguousarray(w.reshape(K_ // 128, 128, nt, wt).transpose(2, 1, 0, 3)).reshape(nt, 128, (K_ // 128) * wt)


def _wload(S_, wst, wtb, i, w_tiled, t, kch, width=WT, c0=0, nch=None):
    nch = kch if nch is None else nch
    n = nch * width
    st = wst[i % len(wst)]; wb = wtb[i % len(wtb)]
    S_.dma(st[:, 0:n], w_tiled[t, :, c0 * width:c0 * width + n])
    S_.vec("tensor_copy", wb[:, 0:n], st[:, 0:n])
    return wb[:, 0:n].rearrange("p (c n) -> p c n", n=width)


def stage_A(nc, tag, T, NM, NG, xT, gain, wm, wg, bg, pm, pg):
    ntm = -(-NM // WT); ntg = -(-NG // WT)
    NH = T // 512
    with contextlib.ExitStack() as es:
        sb = lambda n, s, d: es.enter_context(nc.sbuf_tensor(n, s, d))
        ps = lambda n, s, d: es.enter_context(nc.psum_tensor(n, s, d))
        x32 = sb("x32", [128, KC, T], F32)
        sq = sb("sq", [128, KC, 512], BF16)
        hT = sb("hT", [128, KC, T], BF16)
        g_sb = sb("g_sb", [128, KC], F32)
        bg_sb = sb("bg_sb", [128, NG // 128], F32)
        ones = sb("ones", [128, 128], BF16)
        rstd = sb("rstd", [128, T], F32)
        wst = [sb(f"wst{i}", [128, KC * WT], F32) for i in range(2)]
        wtb = [sb(f"wtb{i}", [128, KC * WT], BF16) for i in range(2)]
        o32 = [sb(f"o32_{i}", [128, 512], F32) for i in range(3)]
        acc = [ps(f"acc{i}", [128, 512], F32) for i in range(3)]
        accn = ps("accn", [128, 512], F32)
        S_ = Serial(nc, tag)
        S_.dma(x32[:], xT.rearrange("(c p) t -> p c t", p=128))
        S_.dma(g_sb[:], gain); S_.dma(bg_sb[:], bg)
        S_.vec("memset", ones[:], 1.0)
        for h in range(NH):
            ts = slice(h * 512, (h + 1) * 512)
            S_.act(sq[:], x32[:, :, ts], AF.Square)
            for c in range(KC):
                S_.mm(accn[:], ones[:], sq[:, c, :], start=(c == 0), stop=(c == KC - 1))
            S_.vec("tensor_scalar", rstd[:, ts], accn[:], 1.0 / D, EPS, ALU.mult, ALU.add)
        S_.act(rstd[:], rstd[:], AF.Sqrt)
        S_.vec("reciprocal", rstd[:], rstd[:])
        for c in range(KC):
            S_.vec("scalar_tensor_tensor", hT[:, c, :], x32[:, c, :], g_sb[:, c:c + 1], rstd[:], ALU.mult, ALU.mult)
        it = 0; ib = 0
        for (w, out, N, gate, nt) in ((wm, pm, NM, False, ntm), (wg, pg, NG, True, ntg)):
            for t in range(nt):
                wv = _wload(S_, wst, wtb, it, w, t, KC); it += 1
                for jb in range(WT // 128):
                    j0 = t * WT + jb * 128
                    wj = min(128, N - j0)
                    if wj <= 0:
                        continue
                    for h in range(NH):
                        ts = slice(h * 512, (h + 1) * 512)
                        a = acc[ib % 3]; o = o32[ib % 3]; ib += 1
                        for c in range(KC):
                            S_.mm(a[0:wj, :], wv[:, c, jb * 128:jb * 128 + wj], hT[:, c, ts], start=(c == 0), stop=(c == KC - 1))
                        if gate:
                            S_.act(o[0:wj, :], a[0:wj, :], AF.Sigmoid, bias=bg_sb[0:wj, j0 // 128:j0 // 128 + 1])
                        else:
                            S_.act(o[0:wj, :], a[0:wj, :], AF.Copy)
                        S_.dma(out[j0:j0 + wj, ts], o[0:wj, :], eng="gpsimd")
        S_.emit()


def build_A(T, NM, NG):
    nc = bass.Bass("TRN2", target_bir_lowering=False)
    ntm = -(-NM // WT); ntg = -(-NG // WT)
    xT = nc.dram_tensor("xT", [D, T], F32, kind="ExternalInput").ap()
    gain = nc.dram_tensor("gain", [128, KC], F32, kind="ExternalInput").ap()
    wm = nc.dram_tensor("wm", [ntm, 128, KC * WT], F32, kind="ExternalInput").ap()
    wg = nc.dram_tensor("wg", [ntg, 128, KC * WT], F32, kind="ExternalInput").ap()
    bg = nc.dram_tensor("bg", [128, NG // 128], F32, kind="ExternalInput").ap()
    pm = nc.dram_tensor("pm", [NM, T], F32, kind="ExternalOutput").ap()
    pg = nc.dram_tensor("pg", [NG, T], F32, kind="ExternalOutput").ap()
    NH = T // 512
    stage_A(nc, "a", T, NM, NG, xT, gain, wm, wg, bg, pm, pg)
    return nc


def sb_consts():
    j = np.arange(128)
    L = (j[:, None] >= j[None, :]).astype(np.float32)
    tl = np.arange(512)
    M = np.stack([(128 * r + j[:, None] < tl[None, :]).astype(np.float32) for r in range(4)], 1)
    return L, M


def emit_sb(nc, S_, es, qT, kT, v, Lc, Mc, yT, SEQ):
    sb = lambda n, s, d: es.enter_context(nc.sbuf_tensor(n, s, d))
    ps = lambda n, s, d: es.enter_context(nc.psum_tensor(n, s, d))
    NB = SEQ // 128
    NG = SEQ // 512
    scale = 128 ** -0.5
    st32 = sb("sb_st32", [128, 2048], F32)
    qb = sb("sb_qb", [128, SEQ], BF16)
    qnb = sb("sb_qnb", [128, SEQ], BF16)
    kb = sb("sb_kb", [128, SEQ], BF16)
    vb = sb("sb_vb", [128, NB, 128], BF16)
    Lb = sb("sb_L", [128, 128], BF16)
    onesb = sb("sb_ones", [128, 128], BF16)
    Mb = sb("sb_M", [128, 4, 512], BF16)
    carry = sb("sb_carry", [128, 512], F32)
    e32 = [sb(f"sb_e{i}", [128, 512], F32) for i in range(2)]
    spb = [sb(f"sb_sp{i}", [128, 512], BF16) for i in range(3)]
    tmp = [sb(f"sb_tmp{i}", [128, 512], F32) for i in range(2)]
    wb = [sb(f"sb_w{i}", [128, 512], BF16) for i in range(3)]
    o32 = [sb(f"sb_o{i}", [128, 512], F32) for i in range(2)]
    Z = [ps(f"sb_Z{i}", [128, 512], F32) for i in range(2)]
    C = [ps(f"sb_C{i}", [128, 512], F32) for i in range(2)]
    Tt = [ps(f"sb_T{i}", [128, 512], F32) for i in range(2)]
    O = [ps(f"sb_O{i}", [128, 512], F32) for i in range(2)]
    for t0 in range(0, SEQ, 2048):
        n = min(2048, SEQ - t0)
        S_.dma(st32[:, 0:n], qT[:, t0:t0 + n])
        S_.act(qb[:, t0:t0 + n], st32[:, 0:n], AF.Copy, scale=scale)
        S_.act(qnb[:, t0:t0 + n], st32[:, 0:n], AF.Copy, scale=-scale)
    S_.dma_cast(kb[:], kT)
    S_.dma_cast(vb[:], v.rearrange("(b p) d -> p b d", p=128))
    S_.dma_cast(Lb[:], Lc)
    S_.dma_cast(Mb[:], Mc)
    S_.vec("memset", onesb[:], 1.0)
    tiles = [(G, kbi) for G in range(NG) for kbi in range(4 * G + 3, -1, -1)]
    NT = len(tiles)

    def st1(i):
        G, kbi = tiles[i]
        qs = slice(G * 512, (G + 1) * 512); ks = slice(kbi * 128, (kbi + 1) * 128)
        S_.mm(Z[i % 2][:], kb[:, ks], qb[:, qs])
        S_.act(e32[i % 2][:], Z[i % 2][:], AF.Exp)
        S_.act(spb[i % 3][:], e32[i % 2][:], AF.Ln, bias=1.0)
        if kbi >= 4 * G:
            S_.vec("tensor_tensor", spb[i % 3][:], spb[i % 3][:], Mb[:, kbi - 4 * G, :], ALU.mult, eng="gpsimd")

    def st2(i):
        G, kbi = tiles[i]
        qs = slice(G * 512, (G + 1) * 512); ks = slice(kbi * 128, (kbi + 1) * 128)
        S_.mm(C[i % 2][:], Lb[:], spb[i % 3][:], start=True, stop=False)
        S_.mm(C[i % 2][:], kb[:, ks], qnb[:, qs], start=False, stop=True)
        S_.mm(Tt[i % 2][:], onesb[:], spb[i % 3][:])
        if kbi == 4 * G + 3:
            S_.vec("tensor_copy", tmp[i % 2][:], C[i % 2][:])
            S_.vec("tensor_copy", carry[:], Tt[i % 2][:])
        else:
            S_.vec("tensor_tensor", tmp[i % 2][:], C[i % 2][:], carry[:], ALU.add)
            S_.vec("tensor_tensor", carry[:], Tt[i % 2][:], carry[:], ALU.add)
        S_.act(wb[i % 3][:], tmp[i % 2][:], AF.Exp, scale=-1.0)
        if kbi >= 4 * G:
            S_.vec("tensor_tensor", wb[i % 3][:], wb[i % 3][:], Mb[:, kbi - 4 * G, :], ALU.mult, eng="gpsimd")

    def st3(i):
        G, kbi = tiles[i]
        S_.mm(O[G % 2][:], vb[:, kbi, :], wb[i % 3][:], start=(kbi == 4 * G + 3), stop=(kbi == 0))
        if kbi == 0:
            S_.vec("tensor_copy", o32[G % 2][:], O[G % 2][:])
            S_.dma(yT[:, G * 512:(G + 1) * 512], o32[G % 2][:])

    for step in range(NT + 2):
        if step < NT:
            st1(step)
        if 0 <= step - 1 < NT:
            st2(step - 1)
        if 0 <= step - 2 < NT:
            st3(step - 2)


def build_sb(SEQ):
    nc = bass.Bass("TRN2", target_bir_lowering=False)
    qT = nc.dram_tensor("qT", [128, SEQ], F32, kind="ExternalInput").ap()
    kT = nc.dram_tensor("kT", [128, SEQ], F32, kind="ExternalInput").ap()
    v = nc.dram_tensor("v", [SEQ, 128], F32, kind="ExternalInput").ap()
    Lc = nc.dram_tensor("Lc", [128, 128], F32, kind="ExternalInput").ap()
    Mc = nc.dram_tensor("Mc", [128, 4, 512], F32, kind="ExternalInput").ap()
    yT = nc.dram_tensor("yT", [128, SEQ], F32, kind="ExternalOutput").ap()
    with contextlib.ExitStack() as es:
        S_ = Serial(nc)
        emit_sb(nc, S_, es, qT, kT, v, Lc, Mc, yT, SEQ)
        S_.emit()
    return nc


def ref_sb(q, k, v):
    s = q.shape[0]
    z = (q.astype(np.float64) @ k.astype(np.float64).T) * 128 ** -0.5
    mask = np.arange(s)[None, :] < np.arange(s)[:, None]
    lk = np.where(mask, -np.logaddexp(0, z), 0.0)
    later = np.cumsum(lk[:, ::-1], 1)[:, ::-1] - lk
    w = np.where(mask, np.exp(-np.logaddexp(0, -z) + later), 0.0)
    return w @ v.astype(np.float64)


EPS = 1e-6


def ret_consts(head):
    lg = np.log1p(-np.exp2(-5.0 - head)).astype(np.float32).astype(np.float64)
    sl = np.arange(128)[:, None]; tl = np.arange(512)[None, :]
    E0 = np.exp(lg * (tl - sl))
    Ed = []
    for r in range(4):
        valid = (2 * r + sl // 64) <= (tl // 64)
        Ed.append(np.where(valid, np.exp(lg * np.abs(tl - 128 * r - sl)), 0.0))
    tab = np.stack([E0] + Ed, 1).astype(np.float32)
    half = 64
    invf = (10000.0 ** (-2.0 * np.arange(half, dtype=np.float32) / 128)).astype(np.float32)
    invf = np.broadcast_to(invf[None, :], (128, half)).copy()
    ident = np.eye(128, dtype=np.float32)
    sc = np.broadcast_to(np.exp(lg * 128.0 * np.arange(64))[None, :], (128, 64)).astype(np.float32).copy()
    return tab, invf, ident, sc


def emit_ret(nc, S_, es, q, k, v, g, pos, gnw, tab, invf, ident, sc, y, SEQ):
    sb = lambda n, s, d: es.enter_context(nc.sbuf_tensor(n, s, d))
    ps = lambda n, s, d: es.enter_context(nc.psum_tensor(n, s, d))
    NB = SEQ // 128; NG = SEQ // 512
    NH_ = 2 if NB >= 8 else 1
    HB = NB // NH_
    qa = sb("rt_qa", [128, HB, 128], F32)
    cos = sb("rt_cos", [128, NB, 64], F32)
    sin = sb("rt_sin", [128, NB, 64], F32)
    t1 = sb("rt_t1", [128, NB, 64], F32)
    t2 = sb("rt_t2", [128, NB, 64], F32)
    rot = sb("rt_rot", [128, HB, 128], F32)
    rotb = sb("rt_rotb", [128, HB, 128], BF16)
    qTb = sb("rt_qTb", [128, SEQ], BF16)
    kTb = sb("rt_kTb", [128, SEQ], BF16)
    vb = sb("rt_vb", [128, NB, 128], BF16)
    posi = sb("rt_posi", [128, NB], I32)
    posf = sb("rt_posf", [128, NB], F32)
    invf_sb = sb("rt_invf", [128, 64], F32)
    identb = sb("rt_ident", [128, 128], BF16)
    tab_sb = sb("rt_tab", [128, 5, 512], F32)
    gnw_sb = sb("rt_gnw", [128, 128], F32)
    sc_sb = sb("rt_sc", [128, 64], F32)
    wt = [sb(f"rt_wt{i}", [128, 512], BF16) for i in range(3)]
    osb = [sb(f"rt_osb{j}", [128, 128], F32) for j in range(4)]
    g_sb = sb("rt_g", [128, 4, 128], F32)
    cen = sb("rt_cen", [128, 128], F32)
    sq = sb("rt_sq", [128, 128], F32)
    st = sb("rt_st", [128, 4], F32)
    yo = sb("rt_yo", [128, 4, 128], F32)
    TP = ps("rt_TP", [128, 512], F32)
    Sc = [ps(f"rt_Sc{i}", [128, 512], F32) for i in range(2)]
    Ob = [ps(f"rt_O{j}", [128, 512], F32) for j in range(4)]
    S_.dma(posi[:], pos); S_.dma(invf_sb[:], invf); S_.dma(tab_sb[:], tab); S_.dma(gnw_sb[:], gnw); S_.dma(sc_sb[:], sc)
    S_.dma_cast(identb[:], ident)
    S_.dma_cast(vb[:], v.rearrange("(b p) d -> p b d", p=128))
    S_.vec("tensor_copy", posf[:], posi[:])
    for b in range(NB):
        S_.vec("tensor_scalar", t1[:, b, :], invf_sb[:], posf[:, b:b + 1], None, ALU.mult)
    MAGIC = 12582912.0
    for (dst, shift) in ((sin, 0.0), (cos, 0.5 * math.pi)):
        if shift:
            S_.vec("tensor_scalar", t1[:], t1[:], shift, None, ALU.add)
        S_.vec("tensor_scalar", t2[:], t1[:], 1.0 / (2 * math.pi), MAGIC, ALU.mult, ALU.add)
        S_.vec("tensor_scalar", t2[:], t2[:], -MAGIC, None, ALU.add)
        S_.vec("scalar_tensor_tensor", t2[:], t2[:], -2 * math.pi, t1[:], ALU.mult, ALU.add)
        S_.vec("tensor_scalar", t2[:], t2[:], math.pi, -math.pi, ALU.min, ALU.max)
        S_.act(dst[:], t2[:], AF.Sin)
    for (src, dstT, scl) in ((q, qTb, 1.0), (k, kTb, 128 ** -0.5)):
        for hb_ in range(NH_):
            b0 = hb_ * HB
            bs = slice(b0, b0 + HB)
            S_.dma(qa[:], src[b0 * 128:(b0 + HB) * 128, :].rearrange("(b p) d -> p b d", p=128))
            a1 = qa[:, :, 0:64]; a2 = qa[:, :, 64:128]
            u1 = t1[:, 0:HB, :]; u2 = t2[:, 0:HB, :]
            S_.vec("tensor_tensor", u1, a1, cos[:, bs, :], ALU.mult)
            S_.vec("tensor_tensor", u2, a2, sin[:, bs, :], ALU.mult)
            S_.vec("tensor_tensor", rot[:, :, 0:64], u1, u2, ALU.subtract)
            S_.vec("tensor_tensor", u1, a1, sin[:, bs, :], ALU.mult)
            S_.vec("tensor_tensor", u2, a2, cos[:, bs, :], ALU.mult)
            S_.vec("tensor_tensor", rot[:, :, 64:128], u1, u2, ALU.add)
            S_.act(rotb[:], rot[:], AF.Copy, scale=scl)
            for G in range(HB // 4):
                Gg = b0 // 4 + G
                for j in range(4):
                    S_.mm(TP[:, j * 128:(j + 1) * 128], rotb[:, 4 * G + j, :], identb[:])
                S_.act(dstT[:, Gg * 512:(Gg + 1) * 512], TP[:], AF.Copy)
    tiles = [(G, kbi) for G in range(NG) for kbi in range(0, 4 * G + 4)]
    NT = len(tiles)

    def st1(i):
        G, kbi = tiles[i]
        S_.mm(Sc[i % 2][:], kTb[:, kbi * 128:(kbi + 1) * 128], qTb[:, G * 512:(G + 1) * 512])
        if kbi >= 4 * G:
            S_.vec("tensor_tensor", wt[i % 3][:], Sc[i % 2][:], tab_sb[:, 1 + kbi - 4 * G, :], ALU.mult)
        else:
            dlt = 4 * G - kbi
            S_.vec("scalar_tensor_tensor", wt[i % 3][:], Sc[i % 2][:], sc_sb[:, dlt:dlt + 1], tab_sb[:, 0, :], ALU.mult, ALU.mult)

    def st2(i):
        G, kbi = tiles[i]
        last = 4 * G + 3
        for j in range(4):
            S_.mm(Ob[j][:, 0:128], wt[i % 3][:, j * 128:(j + 1) * 128], vb[:, kbi, :], start=(kbi == 0), stop=(kbi == last))
        if kbi != last:
            return
        for j in range(4):
            S_.act(osb[j][:], Ob[j][:, 0:128], AF.Copy)
        S_.dma(g_sb[:], g[G * 512:(G + 1) * 512, :].rearrange("(b p) d -> p b d", p=128))
        S_.act(g_sb[:], g_sb[:], AF.Silu)
        for j in range(4):
            S_.vec("reduce_sum", st[:, 0:1], osb[j][:], AX.X)
            S_.vec("tensor_scalar", st[:, 0:1], st[:, 0:1], 1.0 / 128, None, ALU.mult)
            S_.vec("tensor_scalar", cen[:], osb[j][:], st[:, 0:1], None, ALU.subtract)
            S_.vec("tensor_tensor", sq[:], cen[:], cen[:], ALU.mult)
            S_.vec("reduce_sum", st[:, 1:2], sq[:], AX.X)
            S_.vec("tensor_scalar", st[:, 1:2], st[:, 1:2], 1.0 / 128, EPS, ALU.mult, ALU.add)
            S_.act(st[:, 2:3], st[:, 1:2], AF.Sqrt)
            S_.vec("reciprocal", st[:, 3:4], st[:, 2:3])
            S_.vec("scalar_tensor_tensor", cen[:], cen[:], st[:, 3:4], gnw_sb[:], ALU.mult, ALU.mult)
            S_.vec("tensor_tensor", yo[:, j, :], cen[:], g_sb[:, j, :], ALU.mult)
        S_.dma(y[G * 512:(G + 1) * 512, :].rearrange("(b p) d -> p b d", p=128), yo[:])

    for step in range(NT + 1):
        if step < NT:
            st1(step)
        if step >= 1:
            st2(step - 1)


def build_ret(SEQ):
    nc = bass.Bass("TRN2", target_bir_lowering=False)
    NB = SEQ // 128
    di = lambda n, s, d=F32: nc.dram_tensor(n, s, d, kind="ExternalInput").ap()
    q = di("q", [SEQ, 128]); k = di("k", [SEQ, 128]); v = di("v", [SEQ, 128]); g = di("g", [SEQ, 128])
    pos = di("pos", [128, NB], I32); gnw = di("gnw", [128, 128]); tab = di("tab", [128, 5, 512])
    invf = di("invf", [128, 64]); ident = di("ident", [128, 128]); sc = di("sc", [128, 64])
    y = nc.dram_tensor("y", [SEQ, 128], F32, kind="ExternalOutput").ap()
    with contextlib.ExitStack() as es:
        S_ = Serial(nc)
        emit_ret(nc, S_, es, q, k, v, g, pos, gnw, tab, invf, ident, sc, y, SEQ)
        S_.emit()
    return nc


def ssd_consts():
    j = np.arange(128)
    tri = (j[:, None] <= j[None, :]).astype(np.float32)
    tl = np.arange(512)
    Mle = np.stack([(128 * r + j[:, None] <= tl[None, :]).astype(np.float32) for r in range(4)], 1)
    return tri, Mle, np.eye(128, dtype=np.float32)


def emit_ssd(nc, S_, es, xT, BT, CT, zT, dtr, cwx, cbx, cwB, cbB, cwC, cbC, dtb, alog, dsk, tri, Mle, ident, yT, SEQ):
    sb = lambda n, s, d: es.enter_context(nc.sbuf_tensor(n, s, d))
    ps = lambda n, s, d: es.enter_context(nc.psum_tensor(n, s, d))
    NB = SEQ // 128; NG = SEQ // 512
    SL = min(SEQ, 4096)
    up = sb("sd_up", [128, 3 + SL], F32)
    acc = sb("sd_acc", [128, SL], F32)
    xc = [sb(f"sd_xc{e}", [64, SEQ], F32) for e in range(2)]
    Bc = sb("sd_Bc", [128, SEQ], BF16)
    Cc = sb("sd_Cc", [128, SEQ], BF16)
    Btok = sb("sd_Btok", [128, NB, 128], BF16)
    cw = sb("sd_cw", [128, 4], F32); cb = sb("sd_cb", [128, 1], F32)
    dt = sb("sd_dt", [128, NB, 2], F32); ev = sb("sd_ev", [128, NB, 2], F32)
    a = sb("sd_a", [128, NB, 2], F32); acl = sb("sd_acl", [128, NB, 2], F32)
    dfac = sb("sd_dfac", [128, NB, 2], F32); extot = sb("sd_extot", [128, NB, 2], F32)
    dtb_sb = sb("sd_dtb", [128, 2], F32); negA = sb("sd_negA", [128, 2], F32)
    dsk_sb = [sb(f"sd_dsk{e}", [64, 1], F32) for e in range(2)]
    tri_sb = sb("sd_tri", [128, 128], F32); ones32 = sb("sd_ones", [128, 128], F32); id32 = sb("sd_id", [128, 128], F32)
    idb = sb("sd_idb", [128, 128], BF16)
    M0 = sb("sd_M0", [128, 128], F32)
    xdt = sb("sd_xdt", [128, NB, 128], BF16)
    xdtd = sb("sd_xdtd", [128, NB, 128], BF16)
    dg = [sb(f"sd_dg{i}", [128, 128], F32) for i in range(4)]
    seg = [sb(f"sd_seg{i}", [128, 128], F32) for i in range(4)]
    exr = [sb(f"sd_exr{i}", [128, 128], F32) for i in range(4)]
    wt = [sb(f"sd_wt{i}", [128, 128], BF16) for i in range(4)]
    cd = [sb(f"sd_cd{i}", [128, 128], BF16) for i in range(4)]
    St = [sb(f"sd_St{e}", [128, 64], F32) for e in range(2)]
    Sb_ = [sb(f"sd_Sb{e}", [128, 64], BF16) for e in range(2)]
    zt = [sb(f"sd_z{e}", [64, 512], F32) for e in range(2)]; yo = [sb(f"sd_yo{e}", [64, 512], F32) for e in range(2)]
    P1 = ps("sd_P1", [128, 512], F32)
    P2 = ps("sd_P2", [128, 512], F32)
    Pcb = ps("sd_Pcb", [128, 512], F32)
    ARl = [ps(f"sd_ARl{i}", [128, 512], F32) for i in range(2)]
    Y = [ps(f"sd_Y{e}", [64, 512], F32) for e in range(2)]
    for (dst, src) in ((tri_sb, tri), (id32, ident), (M0, Mle[:, 0, 0:128]), (dtb_sb, dtb), (negA, alog)):
        S_.dma(dst[:], src)
    S_.dma_cast(idb[:], ident)
    S_.vec("memset", ones32[:], 1.0)

    def conv(src, w, b, np_, lo, dst):
        S_.dma(cw[0:np_, :], w[lo:lo + np_, :]); S_.dma(cb[0:np_, :], b[lo:lo + np_, :])
        for t0 in range(0, SEQ, SL):
            if t0 == 0:
                S_.vec("memset", up[0:np_, 0:3], 0.0)
                S_.dma(up[0:np_, 3:3 + SL], src[lo:lo + np_, 0:SL])
            else:
                S_.dma(up[0:np_, 0:3 + SL], src[lo:lo + np_, t0 - 3:t0 + SL])
            S_.vec("tensor_scalar", acc[0:np_, :], up[0:np_, 0:SL], cw[0:np_, 0:1], None, ALU.mult)
            for k in range(1, 4):
                S_.vec("scalar_tensor_tensor", acc[0:np_, :], up[0:np_, k:k + SL], cw[0:np_, k:k + 1], acc[0:np_, :], ALU.mult, ALU.add)
            S_.act(dst[0:np_, t0:t0 + SL], acc[0:np_, :], AF.Silu, bias=cb[0:np_, :])
    for e in range(2):
        conv(xT, cwx, cbx, 64, 64 * e, xc[e])
        S_.dma(dsk_sb[e][:], dsk[64 * e:64 * e + 64, :])
    conv(BT, cwB, cbB, 128, 0, Bc)
    conv(CT, cwC, cbC, 128, 0, Cc)
    S_.dma(dt[:], dtr)
    for e in range(2):
        S_.vec("tensor_scalar", dt[:, :, e], dt[:, :, e], dtb_sb[:, e:e + 1], None, ALU.add)
    S_.act(ev[:], dt[:], AF.Exp)
    S_.act(dt[:], ev[:], AF.Ln, bias=1.0)
    S_.act(negA[:], negA[:], AF.Exp)
    S_.vec("tensor_scalar", negA[:], negA[:], -1.0, None, ALU.mult)
    for e in range(2):
        S_.vec("tensor_scalar", a[:, :, e], dt[:, :, e], negA[:, e:e + 1], None, ALU.mult)
    fl = lambda t: t[:].rearrange("p b e -> p (b e)")
    S_.mm(P1[:, 0:NB * 2], tri_sb[:], fl(a))
    S_.mm(P2[:, 0:NB * 2], ones32[:], fl(a))
    S_.vec("tensor_copy", fl(acl), P1[:, 0:NB * 2])
    S_.vec("tensor_copy", fl(ev), P2[:, 0:NB * 2])
    S_.vec("tensor_tensor", fl(dfac), fl(ev), fl(acl), ALU.subtract)
    S_.act(fl(dfac), fl(dfac), AF.Exp)
    S_.act(fl(extot), fl(ev), AF.Exp)
    for b in range(NB):
        for e in range(2):
            S_.mm(P1[:, 64 * e:64 * e + 64], xc[e][:, b * 128:(b + 1) * 128], id32[0:64, 0:64])
        S_.mm(P2[:, 0:128], Bc[:, b * 128:(b + 1) * 128], idb[:])
        for e in range(2):
            S_.vec("tensor_scalar", xdt[:, b, 64 * e:64 * e + 64], P1[:, 64 * e:64 * e + 64], dt[:, b, e:e + 1], None, ALU.mult)
            S_.vec("tensor_scalar", xdtd[:, b, 64 * e:64 * e + 64], P1[:, 64 * e:64 * e + 64], dt[:, b, e:e + 1], dfac[:, b, e:e + 1], ALU.mult, ALU.mult)
        S_.act(Btok[:, b, :], P2[:, 0:128], AF.Copy)
    for e in range(2):
        S_.vec("memset", St[e][:], 0.0)
        S_.vec("memset", Sb_[e][:], 0.0)
    def pre(b):
        bs = slice(b * 128, (b + 1) * 128)
        pc = Pcb[:, (b % 2) * 128:(b % 2) * 128 + 128]
        ix = [2 * (b % 2) + e for e in range(2)]
        ar = [ARl[b % 2][:, e * 128:(e + 1) * 128] for e in range(2)]
        S_.mm(pc, Bc[:, bs], Cc[:, bs])
        for e in range(2):
            S_.vec("tensor_scalar", dg[ix[e]][:], id32[:], acl[:, b, e:e + 1], None, ALU.mult)
        for e in range(2):
            S_.mm(ar[e], ones32[:], dg[ix[e]][:])
        for e in range(2):
            S_.vec("tensor_scalar", seg[ix[e]][:], ar[e], acl[:, b, e:e + 1], 0.0, ALU.subtract, ALU.min)
        for e in range(2):
            S_.act(seg[ix[e]][:], seg[ix[e]][:], AF.Exp)
            S_.act(exr[ix[e]][:], ar[e], AF.Exp)
        for e in range(2):
            S_.vec("tensor_tensor", seg[ix[e]][:], seg[ix[e]][:], M0[:], ALU.mult)
        for e in range(2):
            S_.vec("tensor_tensor", wt[ix[e]][:], pc, seg[ix[e]][:], ALU.mult)
            S_.vec("tensor_tensor", cd[ix[e]][:], Cc[:, bs], exr[ix[e]][:], ALU.mult)

    def post(b):
        G, j = divmod(b, 4)
        ix = [2 * (b % 2) + e for e in range(2)]
        for e in range(2):
            yv = Y[e][:, j * 128:(j + 1) * 128]
            S_.mm(yv, xdt[:, b, 64 * e:64 * e + 64], wt[ix[e]][:], start=True, stop=False)
            S_.mm(yv, Sb_[e][:], cd[ix[e]][:], start=False, stop=True)
            S_.mm(P1[:, 64 * e:64 * e + 64], Btok[:, b, :], xdtd[:, b, 64 * e:64 * e + 64])
        for e in range(2):
            S_.vec("scalar_tensor_tensor", St[e][:], St[e][:], extot[:, b, e:e + 1], P1[:, 64 * e:64 * e + 64], ALU.mult, ALU.add)
        for e in range(2):
            S_.act(Sb_[e][:], St[e][:], AF.Copy)
        if j == 3:
            qs = slice(G * 512, (G + 1) * 512)
            for e in range(2):
                S_.dma(zt[e][:], zT[64 * e:64 * e + 64, qs])
                S_.act(zt[e][:], zt[e][:], AF.Silu)
                S_.vec("scalar_tensor_tensor", yo[e][:], xc[e][:, qs], dsk_sb[e][:, 0:1], Y[e][:], ALU.mult, ALU.add)
                S_.vec("tensor_tensor", yo[e][:], yo[e][:], zt[e][:], ALU.mult)
                S_.dma(yT[64 * e:64 * e + 64, qs], yo[e][:])

    pre(0)
    for b in range(NB):
        if b + 1 < NB:
            pre(b + 1)
        post(b)


def build_ssd(SEQ):
    nc = bass.Bass("TRN2", target_bir_lowering=False)
    NB = SEQ // 128
    di = lambda n, s, d=F32: nc.dram_tensor(n, s, d, kind="ExternalInput").ap()
    args = [di("xT", [128, SEQ]), di("BT", [128, SEQ]), di("CT", [128, SEQ]), di("zT", [128, SEQ]), di("dtr", [128, NB, 2]),
            di("cwx", [128, 4]), di("cbx", [128, 1]), di("cwB", [128, 4]), di("cbB", [128, 1]), di("cwC", [128, 4]), di("cbC", [128, 1]),
            di("dtb", [128, 2]), di("alog", [128, 2]), di("dsk", [128, 1]), di("tri", [128, 128]), di("Mle", [128, 4, 512]), di("ident", [128, 128])]
    yT = nc.dram_tensor("yT", [128, SEQ], F32, kind="ExternalOutput").ap()
    with contextlib.ExitStack() as es:
        S_ = Serial(nc)
        emit_ssd(nc, S_, es, *args, yT, SEQ)
        S_.emit()
    return nc


def ssd_inputs(c, z, xbc, dtraw, conv_w, conv_b, dt_bias, a_log, d_skip, SEQ):
    g = c // 2
    ch = slice(128 * c, 128 * c + 128); Bs = slice(1024 + 128 * g, 1024 + 128 * g + 128); Cs = slice(1536 + 128 * g, 1536 + 128 * g + 128)
    T = lambda a: np.ascontiguousarray(a.T)
    rep = lambda v: np.broadcast_to(v[None, :], (128, v.shape[0])).copy()
    tri, Mle, ident = ssd_consts()
    return {"xT": T(xbc[:, ch]), "BT": T(xbc[:, Bs]), "CT": T(xbc[:, Cs]), "zT": T(z[:, ch]),
            "dtr": np.ascontiguousarray(dtraw[:, 2 * c:2 * c + 2].reshape(SEQ // 128, 128, 2).transpose(1, 0, 2)),
            "cwx": T(conv_w[:, ch]), "cbx": conv_b[ch].reshape(128, 1).copy(), "cwB": T(conv_w[:, Bs]), "cbB": conv_b[Bs].reshape(128, 1).copy(),
            "cwC": T(conv_w[:, Cs]), "cbC": conv_b[Cs].reshape(128, 1).copy(),
            "dtb": rep(dt_bias[2 * c:2 * c + 2]), "alog": rep(a_log[2 * c:2 * c + 2]),
            "dsk": np.repeat(d_skip[2 * c:2 * c + 2], 64).reshape(128, 1).copy(), "tri": tri, "Mle": Mle, "ident": ident}


def _rstd_from(S_, ones, acc, sqt, nch, dn, rstd):
    for c in range(nch):
        S_.mm(acc[:], ones[:], sqt[:, c, :], start=(c == 0), stop=(c == nch - 1))
    S_.vec("tensor_scalar", rstd[:], acc[:], 1.0 / dn, 1e-6, ALU.mult, ALU.add)
    S_.act(rstd[:], rstd[:], AF.Sqrt)
    S_.vec("reciprocal", rstd[:], rstd[:])


def stage_C1(nc, tag, T, xT, pg, ys, wbr, wo, nw, nmp, xo):
    assert T == 1024
    with contextlib.ExitStack() as outer:
        mg = outer.enter_context(nc.sbuf_tensor("c1_mg", [128, KC, T], BF16))
        with contextlib.ExitStack() as es:
            sb = lambda n, s, d: es.enter_context(nc.sbuf_tensor(n, s, d))
            ps = lambda n, s, d: es.enter_context(nc.psum_tensor(n, s, d))
            yb = [sb(f"c1a_y{i}", [128, 8, T], BF16) for i in range(3)]
            sq = sb("c1a_sq", [128, 8, 512], BF16)
            m32 = [[sb(f"c1a_m32_{jb}_{h}", [128, 512], F32) for h in range(2)] for jb in range(2)]
            tmp = [sb(f"c1a_tmp{i}", [128, 512], F32) for i in range(2)]
            gt = [sb(f"c1a_gt{i}", [128, 512], F32) for i in range(4)]
            wst = [sb(f"c1a_wst{i}", [128, 8 * WT], F32) for i in range(2)]
            wtb = [sb(f"c1a_wtb{i}", [128, 8 * WT], BF16) for i in range(2)]
            ones = sb("c1a_ones", [128, 128], BF16)
            rstd = sb("c1a_rstd", [128, 512], F32)
            nw_sb = sb("c1a_nw", [128, 8], F32)
            acc = [ps(f"c1a_acc{i}", [128, 512], F32) for i in range(4)]
            acc2 = ps("c1a_accn", [128, 512], F32)
            S_ = Serial(nc, tag + "a")
            S_.vec("memset", ones[:], 1.0)
            S_.dma(nw_sb[:], nw)
            for i in range(3):
                S_.dma_cast(yb[i][:], ys[i].rearrange("(c p) t -> p c t", p=128))
            for h in range(2):
                ts = slice(h * 512, (h + 1) * 512)
                S_.act(sq[:], yb[2][:, :, ts], AF.Square)
                _rstd_from(S_, ones, acc2, sq, 8, 1024.0, rstd)
                for c in range(8):
                    S_.vec("scalar_tensor_tensor", yb[2][:, c, ts], yb[2][:, c, ts], nw_sb[:, c:c + 1], rstd[:], ALU.mult, ALU.mult)
            it = 0; ib = 0; ig = 0; itmp = 0
            for t in range(D // WT):
                for i in range(3):
                    wv = _wload(S_, wst, wtb, it, wbr[i], t, 8); it += 1
                    for jb in range(WT // 128):
                        j = t * (WT // 128) + jb
                        for h in range(2):
                            ts = slice(h * 512, (h + 1) * 512)
                            a = acc[ib % 4]; ib += 1
                            for c in range(8):
                                S_.mm(a[:], wv[:, c, jb * 128:(jb + 1) * 128], yb[i][:, c, ts], start=(c == 0), stop=(c == 7))
                            g_ = gt[ig % 4]; ig += 1
                            S_.dma(g_[:], pg[i * D + j * 128:i * D + (j + 1) * 128, ts], eng="gpsimd")
                            m_ = m32[jb][h]
                            if i == 0:
                                S_.vec("tensor_tensor", m_[:], a[:], g_[:], ALU.mult)
                            else:
                                t_ = tmp[itmp % 2]; itmp += 1
                                S_.vec("tensor_tensor", t_[:], a[:], g_[:], ALU.mult)
                                S_.vec("tensor_tensor", m_[:], m_[:], t_[:], ALU.add)
                            if i == 2:
                                S_.act(mg[:, j, ts], m_[:], AF.Copy)
            S_.emit()
        with contextlib.ExitStack() as es:
            sb = lambda n, s, d: es.enter_context(nc.sbuf_tensor(n, s, d))
            ps = lambda n, s, d: es.enter_context(nc.psum_tensor(n, s, d))
            o32 = [sb(f"c1b_o{h}", [128, KC, 512], F32) for h in range(2)]
            x32 = sb("c1b_x", [128, KC, 512], F32)
            sq = sb("c1b_sq", [128, KC, 512], BF16)
            wst = [sb(f"c1b_wst{i}", [128, KC * WT], F32) for i in range(2)]
            wtb = [sb(f"c1b_wtb{i}", [128, KC * WT], BF16) for i in range(2)]
            ones = sb("c1b_ones", [128, 128], BF16)
            rstd = sb("c1b_rstd", [128, 512], F32)
            nmp_sb = sb("c1b_nmp", [128, KC], F32)
            acc = [ps(f"c1b_acc{i}", [128, 512], F32) for i in range(4)]
            acc2 = ps("c1b_accn", [128, 512], F32)
            S_ = Serial(nc, tag + "b")
            S_.vec("memset", ones[:], 1.0)
            S_.dma(nmp_sb[:], nmp)
            it = 0; ib = 0
            for t in range(D // WT):
                wv_ = _wload(S_, wst, wtb, it, wo, t, KC); it += 1
                for jb in range(WT // 128):
                    j = t * (WT // 128) + jb
                    for h in range(2):
                        a = acc[ib % 4]; ib += 1
                        for c in range(KC):
                            S_.mm(a[:], wv_[:, c, jb * 128:(jb + 1) * 128], mg[:, c, h * 512:(h + 1) * 512], start=(c == 0), stop=(c == KC - 1))
                        S_.act(o32[h][:, j, :], a[:], AF.Copy)
            for h in range(2):
                ts = slice(h * 512, (h + 1) * 512)
                S_.dma(x32[:], xT[:, ts].rearrange("(c p) t -> p c t", p=128))
                S_.act(sq[:], o32[h][:], AF.Square)
                _rstd_from(S_, ones, acc2, sq, KC, float(D), rstd)
                for c in range(KC):
                    S_.vec("scalar_tensor_tensor", o32[h][:, c, :], o32[h][:, c, :], nmp_sb[:, c:c + 1], rstd[:], ALU.mult, ALU.mult)
                    S_.vec("tensor_tensor", x32[:, c, :], x32[:, c, :], o32[h][:, c, :], ALU.add)
                S_.dma(xo[:, ts].rearrange("(c p) t -> p c t", p=128), x32[:])
            S_.emit()


def build_C1(T):
    nc = bass.Bass("TRN2", target_bir_lowering=False)
    di = lambda n, s, d=F32: nc.dram_tensor(n, s, d, kind="ExternalInput").ap()
    xT = di("xT", [D, T]); pg = di("pg", [3 * D, T])
    ys = [di("yr", [1024, T]), di("ys", [1024, T]), di("yd", [1024, T])]
    wbr = [di(f"wbr{i}", [D // WT, 128, 8 * WT]) for i in range(3)]
    wo = di("wo", [D // WT, 128, KC * WT]); nw = di("nw", [128, 8]); nmp = di("nmp", [128, KC])
    xo = nc.dram_tensor("xo", [D, T], F32, kind="ExternalOutput").ap()
    stage_C1(nc, "c1", T, xT, pg, ys, wbr, wo, nw, nmp, xo)
    return nc


FH = 5632
FC = FH // 128


def stage_C2(nc, tag, T, xT, wg, wu, wd, nfp, nfo, xo):
    HK_unused = None
    with contextlib.ExitStack() as es:
        sb = lambda n, s, d: es.enter_context(nc.sbuf_tensor(n, s, d))
        ps = lambda n, s, d: es.enter_context(nc.psum_tensor(n, s, d))
        x32 = sb("c2_x", [128, KC, 512], F32)
        sq = sb("c2_sq", [128, KC, 512], BF16)
        hT = sb("c2_h", [128, KC, 512], BF16)
        aT = sb("c2_a", [128, FC, 512], BF16)
        sg = [sb(f"c2_sg{i}", [128, 512], F32) for i in range(2)]
        o32 = sb("c2_o", [128, KC, 512], F32)
        wst = [sb(f"c2_wst{i}", [128, KC * WT], F32) for i in range(2)]
        wtb = [sb(f"c2_wtb{i}", [128, KC * WT], BF16) for i in range(2)]
        ones = sb("c2_ones", [128, 128], BF16)
        rstd = sb("c2_rstd", [128, 512], F32)
        nfp_sb = sb("c2_nfp", [128, KC], F32); nfo_sb = sb("c2_nfo", [128, KC], F32)
        accg = [ps(f"c2_accg{i}", [128, 512], F32) for i in range(2)]
        accu = [ps(f"c2_accu{i}", [128, 512], F32) for i in range(2)]
        accd = [ps(f"c2_accd{i}", [128, 512], F32) for i in range(2)]
        acc2 = ps("c2_acc2", [128, 512], F32)
        S_ = Serial(nc, tag)
        S_.vec("memset", ones[:], 1.0)
        S_.dma(nfp_sb[:], nfp); S_.dma(nfo_sb[:], nfo)
        it = 0; ib = 0
        HK = FC // 2
        for h in range(T // 512):
            ts = slice(h * 512, (h + 1) * 512)
            S_.dma(x32[:], xT[:, ts].rearrange("(c p) t -> p c t", p=128))
            S_.act(sq[:], x32[:], AF.Square)
            _rstd_from(S_, ones, acc2, sq, KC, float(D), rstd)
            for c in range(KC):
                S_.vec("scalar_tensor_tensor", hT[:, c, :], x32[:, c, :], nfp_sb[:, c:c + 1], rstd[:], ALU.mult, ALU.mult)
            for t in range(FH // WT):
                wgv = _wload(S_, wst, wtb, it, wg, t, KC); it += 1
                ags = []
                for jb in range(WT // 128):
                    a = accg[jb]
                    for c in range(KC):
                        S_.mm(a[:], wgv[:, c, jb * 128:(jb + 1) * 128], hT[:, c, :], start=(c == 0), stop=(c == KC - 1))
                    S_.act(sg[jb][:], a[:], AF.Silu)
                wuv = _wload(S_, wst, wtb, it, wu, t, KC); it += 1
                for jb in range(WT // 128):
                    m = t * (WT // 128) + jb
                    a = accu[jb]
                    for c in range(KC):
                        S_.mm(a[:], wuv[:, c, jb * 128:(jb + 1) * 128], hT[:, c, :], start=(c == 0), stop=(c == KC - 1))
                    S_.vec("tensor_tensor", aT[:, m, :], a[:], sg[jb][:], ALU.mult)
            for j in range(KC):
                a = accd[j % 2]
                for hk in range(2):
                    wdv = _wload(S_, wst, wtb, it, wd, j, FC, width=128, c0=hk * HK, nch=HK); it += 1
                    for m in range(HK):
                        mm_ = hk * HK + m
                        S_.mm(a[:], wdv[:, m, :], aT[:, mm_, :], start=(mm_ == 0), stop=(mm_ == FC - 1))
                S_.act(o32[:, j, :], a[:], AF.Copy)
            S_.act(sq[:], o32[:], AF.Square)
            _rstd_from(S_, ones, acc2, sq, KC, float(D), rstd)
            for c in range(KC):
                S_.vec("scalar_tensor_tensor", o32[:, c, :], o32[:, c, :], nfo_sb[:, c:c + 1], rstd[:], ALU.mult, ALU.mult)
                S_.vec("tensor_tensor", x32[:, c, :], x32[:, c, :], o32[:, c, :], ALU.add)
            S_.dma(xo[:, ts].rearrange("(c p) t -> p c t", p=128), x32[:])
        S_.emit()


def build_C2(T):
    nc = bass.Bass("TRN2", target_bir_lowering=False)
    di = lambda n, s, d=F32: nc.dram_tensor(n, s, d, kind="ExternalInput").ap()
    xT = di("xT", [D, T]); wg = di("wg", [FH // WT, 128, KC * WT]); wu = di("wu", [FH // WT, 128, KC * WT])
    wd = di("wd", [D // 128, 128, FC * 128])
    nfp = di("nfp", [128, KC]); nfo = di("nfo", [128, KC])
    xo = nc.dram_tensor("xo", [D, T], F32, kind="ExternalOutput").ap()
    stage_C2(nc, "c2", T, xT, wg, wu, wd, nfp, nfo, xo)
    return nc


SEQ = 8192
NCORE = 8
TPC = SEQ // NCORE
NMIX = 10256
_PROG = {}


def _prog(name, fn):
    if name not in _PROG:
        _PROG[name] = fn()
    return _PROG[name]


def _run(nc, maps):
    return run_bass_kernel_spmd(nc, maps, core_ids=list(range(NCORE))).results


def _pc(v, n):
    return np.ascontiguousarray(np.asarray(v, np.float32).reshape(n, 128).T)


def build_tail(T, with_A):
    nc = bass.Bass("TRN2", target_bir_lowering=False)
    di = lambda n, s, d=F32: nc.dram_tensor(n, s, d, kind="ExternalInput").ap()
    xT = di("xT", [D, T]); pg = di("pg_in", [3 * D, T])
    ys = [di("yr", [1024, T]), di("ys", [1024, T]), di("yd", [1024, T])]
    wbr = [di(f"wbr{i}", [D // WT, 128, 8 * WT]) for i in range(3)]
    wo = di("wo", [D // WT, 128, KC * WT]); nw = di("nw", [128, 8]); nmp = di("nmp", [128, KC])
    fg = di("fg", [FH // WT, 128, KC * WT]); fu = di("fu", [FH // WT, 128, KC * WT]); fd = di("fd", [D // 128, 128, FC * 128])
    nfp = di("nfp", [128, KC]); nfo = di("nfo", [128, KC])
    xmid = nc.dram_tensor("xmid", [D, T], F32).ap()
    xo = nc.dram_tensor("xo", [D, T], F32, kind="ExternalOutput").ap()
    stage_C1(nc, "c1", T, xT, pg, ys, wbr, wo, nw, nmp, xmid)
    stage_C2(nc, "c2", T, xmid, fg, fu, fd, nfp, nfo, xo)
    if with_A:
        NM, NG = NMIX, 3 * D
        ntm = -(-NM // WT); ntg = -(-NG // WT)
        gain = di("gain", [128, KC]); wm = di("wm", [ntm, 128, KC * WT]); wg = di("wg", [ntg, 128, KC * WT]); bg = di("bg", [128, NG // 128])
        pm = nc.dram_tensor("pm", [NM, T], F32, kind="ExternalOutput").ap()
        pgo = nc.dram_tensor("pg", [NG, T], F32, kind="ExternalOutput").ap()
        stage_A(nc, "a", T, NM, NG, xo, gain, wm, wg, bg, pm, pgo)
    return nc


def build_mix(SEQ):
    nc = bass.Bass("TRN2", target_bir_lowering=False)
    NB = SEQ // 128
    di = lambda n, s, d=F32: nc.dram_tensor(n, s, d, kind="ExternalInput").ap()
    do = lambda n, s: nc.dram_tensor(n, s, F32, kind="ExternalOutput").ap()
    ident = di("ident", [128, 128])
    sbi = [di("i_sb_qT", [128, SEQ]), di("i_sb_kT", [128, SEQ]), di("i_sb_v", [SEQ, 128]), di("i_sb_Lc", [128, 128]), di("i_sb_Mc", [128, 4, 512])]
    sb_y = do("o_sb_yT", [128, SEQ])
    rti = [di("i_rt_q", [SEQ, 128]), di("i_rt_k", [SEQ, 128]), di("i_rt_v", [SEQ, 128]), di("i_rt_g", [SEQ, 128]), di("i_rt_pos", [128, NB], I32),
           di("i_rt_gnw", [128, 128]), di("i_rt_tab", [128, 5, 512]), di("i_rt_invf", [128, 64])]
    rt_sc = di("i_rt_sc", [128, 64]); rt_y = do("o_rt_y", [SEQ, 128])
    sdi = [di("i_sd_xT", [128, SEQ]), di("i_sd_BT", [128, SEQ]), di("i_sd_CT", [128, SEQ]), di("i_sd_zT", [128, SEQ]), di("i_sd_dtr", [128, NB, 2]),
           di("i_sd_cwx", [128, 4]), di("i_sd_cbx", [128, 1]), di("i_sd_cwB", [128, 4]), di("i_sd_cbB", [128, 1]), di("i_sd_cwC", [128, 4]), di("i_sd_cbC", [128, 1]),
           di("i_sd_dtb", [128, 2]), di("i_sd_alog", [128, 2]), di("i_sd_dsk", [128, 1]), di("i_sd_tri", [128, 128]), di("i_sd_Mle", [128, 4, 512])]
    sd_y = do("o_sd_yT", [128, SEQ])
    with contextlib.ExitStack() as es:
        S_ = Serial(nc, "sb")
        emit_sb(nc, S_, es, *sbi, sb_y, SEQ)
        S_.emit()
    with contextlib.ExitStack() as es:
        S_ = Serial(nc, "rt")
        emit_ret(nc, S_, es, *rti, ident, rt_sc, rt_y, SEQ)
        S_.emit()
    with contextlib.ExitStack() as es:
        S_ = Serial(nc, "sd")
        emit_ssd(nc, S_, es, *sdi, ident, sd_y, SEQ)
        S_.emit()
    return nc


def kernel(x, positions, norm_mix_pre, norm_mix_post, norm_ffn_pre, norm_ffn_post, w_in, b_gate,
           ret_gn_w, ssd_conv_w, ssd_conv_b, ssd_dt_bias, ssd_a_log, ssd_d, ssd_norm_w,
           w_branch_ret, w_branch_sb, w_branch_ssd, w_out, ffn_w_gate, ffn_w_up, ffn_w_down):
    A = lambda a: np.asarray(a)
    C = np.ascontiguousarray
    x = A(x).astype(np.float32, copy=False)
    depth = A(w_in).shape[0]
    xT = C(x[0].T)
    xs = [C(xT[:, c * TPC:(c + 1) * TPC]) for c in range(NCORE)]
    pos_l = C(A(positions)[0].astype(np.int32).reshape(SEQ // 128, 128).T)
    Lc, Mc = sb_consts()
    rc = [ret_consts(c) for c in range(NCORE)]
    pA = _prog("A", lambda: build_A(TPC, NMIX, 3 * D))
    pMX = _prog("MIX", lambda: build_mix(SEQ))
    pT1 = _prog("TAILA", lambda: build_tail(TPC, True))
    pT0 = _prog("TAIL", lambda: build_tail(TPC, False))

    def a_inputs(l):
        wl = A(w_in[l])
        return {"gain": _pc(norm_mix_pre[l], KC), "wm": pretile(wl[:, :NMIX]), "wg": pretile(wl[:, NMIX:]), "bg": _pc(b_gate[l], 3 * D // 128)}

    ai = a_inputs(0)
    rA = _run(pA, [{"xT": xs[c], **ai} for c in range(NCORE)])
    pgs = [rA[c]["pg"] for c in range(NCORE)]
    pT = np.concatenate([rA[c]["pm"] for c in range(NCORE)], axis=1)
    del rA, ai
    for l in range(depth):
        hb = lambda base, c: pT[base + 128 * c: base + 128 * (c + 1)]
        gn = A(ret_gn_w[l])
        z_tm = pT[7168:8192].T; xbc_tm = pT[8192:10240].T; dt_tm = pT[10240:10256].T
        maps = []
        for c in range(NCORE):
            sd = ssd_inputs(c, z_tm, xbc_tm, dt_tm, A(ssd_conv_w[l]), A(ssd_conv_b[l]), A(ssd_dt_bias[l]), A(ssd_a_log[l]), A(ssd_d[l]), SEQ)
            m = {"ident": sd.pop("ident")}
            m.update({"i_sd_" + k_: v_ for k_, v_ in sd.items()})
            m.update({"i_sb_qT": C(hb(4096, c)), "i_sb_kT": C(hb(5120, c)), "i_sb_v": C(hb(6144, c).T), "i_sb_Lc": Lc, "i_sb_Mc": Mc})
            m.update({"i_rt_q": C(hb(0, c).T), "i_rt_k": C(hb(1024, c).T), "i_rt_v": C(hb(2048, c).T), "i_rt_g": C(hb(3072, c).T), "i_rt_pos": pos_l,
                      "i_rt_gnw": np.broadcast_to(gn[128 * c:128 * (c + 1)][None, :], (128, 128)).copy(),
                      "i_rt_tab": rc[c][0], "i_rt_invf": rc[c][1], "i_rt_sc": rc[c][3]})
            maps.append(m)
        rM = _run(pMX, maps)
        ysT = np.concatenate([rM[c]["o_sb_yT"] for c in range(NCORE)], axis=0)
        yrT = np.concatenate([rM[c]["o_rt_y"].T for c in range(NCORE)], axis=0)
        ydT = np.concatenate([rM[c]["o_sd_yT"] for c in range(NCORE)], axis=0)
        del pT, rM, maps
        tk = lambda a, c: C(a[:, c * TPC:(c + 1) * TPC])
        shared = {"wbr0": pretile(A(w_branch_ret[l])), "wbr1": pretile(A(w_branch_sb[l])), "wbr2": pretile(A(w_branch_ssd[l])), "wo": pretile(A(w_out[l])),
                  "nw": _pc(ssd_norm_w[l], 8), "nmp": _pc(norm_mix_post[l], KC),
                  "fg": pretile(A(ffn_w_gate[l])), "fu": pretile(A(ffn_w_up[l])), "fd": pretile(A(ffn_w_down[l]), 128),
                  "nfp": _pc(norm_ffn_pre[l], KC), "nfo": _pc(norm_ffn_post[l], KC)}
        last = (l == depth - 1)
        if not last:
            shared.update(a_inputs(l + 1))
        rT = _run(pT0 if last else pT1, [{"xT": xs[c], "pg_in": pgs[c], "yr": tk(yrT, c), "ys": tk(ysT, c), "yd": tk(ydT, c), **shared} for c in range(NCORE)])
        xs = [rT[c]["xo"] for c in range(NCORE)]
        if not last:
            pgs = [rT[c]["pg"] for c in range(NCORE)]
            pT = np.concatenate([rT[c]["pm"] for c in range(NCORE)], axis=1)
        del rT, shared
    out = np.concatenate(xs, axis=1).T
    return np.ascontiguousarray(out[None]).astype(np.float32)
```
